# Optimizing a Trainium2 kernel written in Bass

```python
import jax, jax.numpy as jnp
from jax import lax
import numpy as np

D_MODEL = 1024
BATCH = 4
SEQ = 8192
DEPTH = 2

N_MIXERS = 2
N_A = (DEPTH + 1) // 2
N_B = DEPTH // 2

M_HEADS = 8
M_QK_DIM = 64
M_V_DIM = D_MODEL // M_HEADS
M_CHUNK = 64
M_F_BIAS_LO = 3.0
M_F_BIAS_HI = 6.0
M_IN_WIDTH = 2 * M_HEADS * M_QK_DIM + 2 * D_MODEL + 4 * M_HEADS

DILATED_GROUPS = ((128, 1), (512, 4), (2048, 16))
N_GROUPS = len(DILATED_GROUPS)
A_HEADS = 8
A_HEAD_DIM = D_MODEL // A_HEADS
A_IN_WIDTH = N_GROUPS * 3 * A_HEADS * A_HEAD_DIM
ROPE_DIM = A_HEAD_DIM // 4
ROPE_THETA = 500000.0
NEG_INF = -1e30

FFN_HIDDEN = -(-(8 * D_MODEL) // (3 * 256)) * 256
PLE_DIM = 256
EPS = 1e-6

kernel_name = "hybrid_mlstm_dilated_attn_encoder"


def rms_norm(x, w):
    xf = x.astype(jnp.float32)
    y = xf * lax.rsqrt(jnp.mean(xf * xf, axis=-1, keepdims=True) + EPS)
    return (y * w.astype(jnp.float32)).astype(x.dtype)


def mlstm_chunkwise(q, k, v, ig, lf):
    B, S, H, Dk = q.shape
    Dv = v.shape[-1]
    L = M_CHUNK
    nc = S // L
    qc = q.reshape(B, nc, L, H, Dk)
    kc = k.reshape(B, nc, L, H, Dk)
    vc = v.reshape(B, nc, L, H, Dv)
    igc = ig.reshape(B, nc, L, H)
    b = jnp.cumsum(lf.reshape(B, nc, L, H), axis=2)
    g = b[:, :, -1]
    a = g[:, :, None, :] - b + igc
    a_max = jnp.max(a, axis=2)

    def step(carry, inp):
        C, n, m = carry
        k_c, v_c, a_c, amax_c, g_c = inp
        m_new = jnp.maximum(g_c + m, amax_c)
        decay = jnp.exp(g_c + m - m_new)
        kw = k_c * jnp.exp(a_c - m_new[:, None, :])[..., None]
        C_new = decay[..., None, None] * C + jnp.einsum('blhk,blhv->bhkv', kw, v_c)
        n_new = decay[..., None] * n + jnp.sum(kw, axis=1)
        return (C_new, n_new, m_new), (C, n, m)

    init = (jnp.zeros((B, H, Dk, Dv), jnp.float32), jnp.zeros((B, H, Dk), jnp.float32),
            jnp.zeros((B, H), jnp.float32))
    xs = (jnp.moveaxis(kc, 1, 0), jnp.moveaxis(vc, 1, 0), jnp.moveaxis(a, 1, 0),
          jnp.moveaxis(a_max, 1, 0), jnp.moveaxis(g, 1, 0))
    _, (C_prev, n_prev, m_prev) = lax.scan(step, init, xs)
    C_prev = jnp.moveaxis(C_prev, 0, 1)
    n_prev = jnp.moveaxis(n_prev, 0, 1)
    m_prev = jnp.moveaxis(m_prev, 0, 1)

    bT = jnp.swapaxes(b, 2, 3)
    igT = jnp.swapaxes(igc, 2, 3)
    D = bT[..., :, None] - bT[..., None, :] + igT[..., None, :]
    tril = jnp.tril(jnp.ones((L, L), dtype=bool))
    D = jnp.where(tril, D, -jnp.inf)
    inter_log = bT + m_prev[..., None]
    m_j = jnp.maximum(inter_log, jnp.max(D, axis=-1))
    Dw = jnp.exp(D - m_j[..., None])
    inter_w = jnp.exp(inter_log - m_j)
    Sm = jnp.einsum('bnlhk,bnshk->bnhls', qc, kc) * Dw
    num = (jnp.einsum('bnhls,bnshv->bnlhv', Sm, vc)
           + jnp.swapaxes(inter_w, 2, 3)[..., None] * jnp.einsum('bnlhk,bnhkv->bnlhv', qc, C_prev))
    den = jnp.sum(Sm, axis=-1) + inter_w * jnp.einsum('bnlhk,bnhk->bnhl', qc, n_prev)
    denom = jnp.maximum(jnp.abs(den), jnp.exp(-m_j))
    h = num / jnp.swapaxes(denom, 2, 3)[..., None]
    return h.reshape(B, S, H, Dv)


def mlstm_mixer(xn, w_in, gate_bias, head_norm, w_out):
    B, S, _ = xn.shape
    H, Dk, Dv = M_HEADS, M_QK_DIM, M_V_DIM
    f32 = jnp.float32
    proj = xn @ w_in
    q, k, v, o, gates = jnp.split(
        proj, [H * Dk, 2 * H * Dk, 2 * H * Dk + D_MODEL, 2 * H * Dk + 2 * D_MODEL], axis=-1)
    q = q.astype(f32).reshape(B, S, H, Dk) * (Dk ** -0.5)
    k = k.astype(f32).reshape(B, S, H, Dk)
    v = v.astype(f32).reshape(B, S, H, Dv)
    gates = (gates.astype(f32) + gate_bias.astype(f32)).reshape(B, S, 4, H)
    ig_f, ig_b = gates[:, :, 0], gates[:, :, 1]
    lf_f = jax.nn.log_sigmoid(gates[:, :, 2])
    lf_b = jax.nn.log_sigmoid(gates[:, :, 3])
    h_fwd = mlstm_chunkwise(q, k, v, ig_f, lf_f)
    flip = lambda t: jnp.flip(t, axis=1)
    h_bwd = flip(mlstm_chunkwise(flip(q), flip(k), flip(v), flip(ig_b), flip(lf_b)))
    h = h_fwd + h_bwd
    h = h * lax.rsqrt(jnp.mean(h * h, axis=-1, keepdims=True) + EPS) * head_norm.astype(f32)
    h = jax.nn.sigmoid(o.astype(f32)) * h.reshape(B, S, H * Dv)
    return h.astype(xn.dtype) @ w_out


def apply_partial_rope(x, cos, sin):
    half = ROPE_DIM // 2
    x1 = x[..., :half]
    x2 = x[..., half:ROPE_DIM]
    rot = jnp.concatenate([x1 * cos - x2 * sin, x2 * cos + x1 * sin], axis=-1).astype(x.dtype)
    return jnp.concatenate([rot, x[..., ROPE_DIM:]], axis=-1)


def dilated_window_attention(q, k, v, dilation, radius):
    B, S, H, Dh = q.shape
    blk = radius
    U = S // dilation
    nb = -(-U // blk)
    Up = nb * blk

    def to_sub(t, front, back):
        t = t.reshape(B, U, dilation, H, Dh)
        return jnp.pad(t, ((0, 0), (front, back), (0, 0), (0, 0), (0, 0)))

    qb = to_sub(q, 0, Up - U).reshape(B, nb, blk, dilation, H, Dh)
    kb = to_sub(k, blk, Up - U + blk).reshape(B, nb + 2, blk, dilation, H, Dh)
    vb = to_sub(v, blk, Up - U + blk).reshape(B, nb + 2, blk, dilation, H, Dh)
    scores = jnp.concatenate(
        [jnp.einsum('bnqrhd,bnkrhd->bnrhqk', qb, kb[:, s:s + nb]) for s in range(3)], axis=-1)
    scores = scores.astype(jnp.float32) * (Dh ** -0.5)
    n_idx = jnp.arange(nb)[:, None, None]
    a_idx = jnp.arange(blk)[None, :, None]
    c_idx = jnp.arange(3 * blk)[None, None, :]
    delta = c_idx - blk - a_idx
    key_u = n_idx * blk - blk + c_idx
    valid = (jnp.abs(delta) <= radius) & (key_u >= 0) & (key_u < U)
    scores = jnp.where(valid[None, :, None, None], scores, NEG_INF)
    lse = jax.nn.logsumexp(scores, axis=-1)
    probs = jnp.exp(scores - lse[..., None]).astype(v.dtype)
    out = jnp.einsum('bnrhqk,bnkrhd->bnqrhd', probs[..., :blk], vb[:, 0:nb])
    for s in range(1, 3):
        out = out + jnp.einsum('bnrhqk,bnkrhd->bnqrhd', probs[..., s * blk:(s + 1) * blk], vb[:, s:s + nb])
    out = out.reshape(B, Up, dilation, H, Dh)[:, :U].reshape(B, S, H, Dh)
    lse = jnp.transpose(lse, (0, 1, 4, 2, 3)).reshape(B, Up, dilation, H)[:, :U].reshape(B, S, H)
    return out, lse


def dilated_mixer(xn, cos, sin, w_in, w_out):
    B, S, _ = xn.shape
    proj = (xn @ w_in).reshape(B, S, N_GROUPS, 3, A_HEADS, A_HEAD_DIM)
    outs, lses = [], []
    for g, (window, dil) in enumerate(DILATED_GROUPS):
        q = apply_partial_rope(proj[:, :, g, 0], cos, sin)
        k = apply_partial_rope(proj[:, :, g, 1], cos, sin)
        o_g, l_g = dilated_window_attention(q, k, proj[:, :, g, 2], dil, window // 2 // dil)
        outs.append(o_g)
        lses.append(l_g)
    w = jax.nn.softmax(jnp.stack(lses, axis=0), axis=0)
    o = jnp.einsum('gbsh,gbshd->bshd', w, jnp.stack(outs, axis=0).astype(jnp.float32))
    return o.reshape(B, S, A_HEADS * A_HEAD_DIM).astype(xn.dtype) @ w_out


def swiglu(xn, w_gate, w_up, w_down):
    return (jax.nn.silu(xn @ w_gate) * (xn @ w_up)) @ w_down


def setup_inputs(seed: int = 0) -> dict:
    key = jax.random.key(seed)
    ks = jax.random.split(key, 24)
    f32 = jnp.float32
    nrm = lambda k, shape, scale: jax.random.normal(k, shape, f32) * scale
    x = nrm(ks[0], (BATCH, SEQ, D_MODEL), 1.0)
    p = nrm(ks[1], (DEPTH, BATCH, SEQ, PLE_DIM), 1.0)
    positions = (jnp.arange(SEQ, dtype=jnp.int32)[None, :]
                 + jax.random.randint(ks[2], (BATCH, 1), 0, 4096, dtype=jnp.int32))
    norm_mix = 1.0 + nrm(ks[3], (DEPTH, D_MODEL), 0.01)
    a_w_in = nrm(ks[4], (N_A, D_MODEL, M_IN_WIDTH), D_MODEL ** -0.5)
    ig_bias = nrm(ks[5], (N_A, 2 * M_HEADS), 0.1)
    fg_bias = (jnp.tile(jnp.linspace(M_F_BIAS_LO, M_F_BIAS_HI, M_HEADS, dtype=f32), (N_A, 2))
               + nrm(ks[6], (N_A, 2 * M_HEADS), 0.01))
    a_gate_bias = jnp.concatenate([ig_bias, fg_bias], axis=-1)
    a_head_norm = 1.0 + nrm(ks[7], (N_A, M_HEADS, M_V_DIM), 0.01)
    a_w_out = nrm(ks[8], (N_A, D_MODEL, D_MODEL), D_MODEL ** -0.5)
    b_w_in = nrm(ks[9], (N_B, D_MODEL, A_IN_WIDTH), D_MODEL ** -0.5)
    b_w_out = nrm(ks[10], (N_B, A_HEADS * A_HEAD_DIM, D_MODEL), (A_HEADS * A_HEAD_DIM) ** -0.5)
    norm_ffn = 1.0 + nrm(ks[11], (DEPTH, D_MODEL), 0.01)
    w_gate = nrm(ks[12], (DEPTH, D_MODEL, FFN_HIDDEN), D_MODEL ** -0.5)
    w_up = nrm(ks[13], (DEPTH, D_MODEL, FFN_HIDDEN), D_MODEL ** -0.5)
    w_down = nrm(ks[14], (DEPTH, FFN_HIDDEN, D_MODEL), FFN_HIDDEN ** -0.5)
    norm_ple = 1.0 + nrm(ks[15], (DEPTH, D_MODEL), 0.01)
    ple_gate = nrm(ks[16], (DEPTH, D_MODEL, D_MODEL), D_MODEL ** -0.5)
    ple_proj = nrm(ks[17], (DEPTH, PLE_DIM, D_MODEL), PLE_DIM ** -0.5)
    final_norm = 1.0 + nrm(ks[18], (D_MODEL,), 0.01)
    return {"x": x, "p": p, "positions": positions, "norm_mix": norm_mix,
            "a_w_in": a_w_in, "a_gate_bias": a_gate_bias, "a_head_norm": a_head_norm, "a_w_out": a_w_out,
            "b_w_in": b_w_in, "b_w_out": b_w_out, "norm_ffn": norm_ffn,
            "w_gate": w_gate, "w_up": w_up, "w_down": w_down,
            "norm_ple": norm_ple, "ple_gate": ple_gate, "ple_proj": ple_proj, "final_norm": final_norm}


def reference(x, p, positions, norm_mix, a_w_in, a_gate_bias, a_head_norm, a_w_out, b_w_in, b_w_out,
              norm_ffn, w_gate, w_up, w_down, norm_ple, ple_gate, ple_proj, final_norm):
    inv_freq = ROPE_THETA ** (-jnp.arange(0, ROPE_DIM, 2, dtype=jnp.float32) / ROPE_DIM)
    angles = positions.astype(jnp.float32)[..., None] * inv_freq
    cos = jnp.cos(angles)[:, :, None, :]
    sin = jnp.sin(angles)[:, :, None, :]
    h = x
    for i in range(DEPTH):
        j = i // N_MIXERS
        hn = rms_norm(h, norm_mix[i])
        if i % N_MIXERS == 0:
            mix = mlstm_mixer(hn, a_w_in[j], a_gate_bias[j], a_head_norm[j], a_w_out[j])
        else:
            mix = dilated_mixer(hn, cos, sin, b_w_in[j], b_w_out[j])
        h = h + mix
        h = h + swiglu(rms_norm(h, norm_ffn[i]), w_gate[i], w_up[i], w_down[i])
        gate = jax.nn.sigmoid(rms_norm(h, norm_ple[i]) @ ple_gate[i])
        h = h + gate * (p[i] @ ple_proj[i])
    return rms_norm(h, final_norm)
```

```python
import contextlib
import os
import numpy as np
import ml_dtypes
import concourse.bass as bass
import concourse.mybir as mybir
from concourse.bass_utils import run_bass_kernel_spmd

F32 = mybir.dt.float32
BF16 = mybir.dt.bfloat16
I32 = mybir.dt.int32
AF = mybir.ActivationFunctionType
ALU = mybir.AluOpType
AX = mybir.AxisListType

D = 1024
SEQ = 8192
OWN = 4096
EXT = 5120
NCH = 8
HID = 2816
NHC = 22
EPS = 1e-6
COMPUTE = ("pe", "act", "dve", "pool")
N_DMA_SEMS = int(os.environ.get("N_DMA_SEMS", "32"))
DMA_ALL_SP = os.environ.get('DMA_ALL_SP', '1') == '1'


class Op:
    __slots__ = ("eng", "fn", "deps", "dma", "pos", "semkey", "semval", "waits", "signal", "vc")

    def __init__(self, eng, fn, dma):
        self.eng = eng
        self.fn = fn
        self.dma = dma
        self.deps = {}
        self.waits = []
        self.signal = False
        self.vc = None


class Prog:
    def __init__(self, nc, stack):
        self.nc = nc
        self.sems = {}
        for i_ in range(int(os.environ.get("SEM_SKIP", "0"))):
            stack.enter_context(nc.semaphore("skip%d" % i_))
        for e in COMPUTE:
            self.sems[e] = stack.enter_context(nc.semaphore("s_" + e))
        for s in range(N_DMA_SEMS):
            self.sems[("d", s)] = stack.enter_context(nc.semaphore("s_d%d" % s))
        self.cur = {k: 0 for k in self.sems}
        self.gates = {e: stack.enter_context(nc.semaphore("g_" + e)) for e in COMPUTE}
        self.phase_no = 0
        self.ndma = 0
        self.n_inst = 0
        self.bodies = {}

    def emit_all(self):
        nc = self.nc
        bodies = self.bodies
        with nc.Block() as block:
            deco = {"pe": block.tensor, "act": block.scalar, "dve": block.vector,
                    "pool": block.gpsimd, "sp": block.sync}
            for eng_name in ("sp", "pe", "act", "dve", "pool"):
                def run(e, lst=bodies.get(eng_name, [])):
                    for f in lst:
                        f(e)
                deco[eng_name](run)


class Phase:
    def __init__(self, prog):
        self.prog = prog
        self.ops = []
        self.lastw = {}
        self.readers = {}
        self.dma_last = {}

    def add(self, eng, fn, reads=(), writes=(), dma=False):
        op = Op(eng, fn, dma)
        idx = len(self.ops)
        for r in reads:
            w = self.lastw.get(r)
            if w is not None:
                op.deps[w] = True
        for wkey in writes:
            w = self.lastw.get(wkey)
            if w is not None and w not in op.deps:
                op.deps[w] = False
            for rd in self.readers.get(wkey, ()):
                if rd not in op.deps:
                    op.deps[rd] = False
        if dma:
            s = self.prog.ndma % N_DMA_SEMS
            self.prog.ndma += 1
            prev = self.dma_last.get(s)
            if prev is not None:
                op.deps[prev] = True
            self.dma_last[s] = idx
            op.semkey = ("d", s)
        else:
            op.semkey = eng
        self.ops.append(op)
        for r in reads:
            self.readers.setdefault(r, []).append(idx)
        for wkey in writes:
            self.lastw[wkey] = idx
            self.readers[wkey] = []
        return idx

    def mm(self, out, lhsT, rhs, start, stop, R, W, **kw):
        self.add("pe", lambda e: e.matmul(out, lhsT=lhsT, rhs=rhs, start=start, stop=stop, **kw), R, W)

    def tr(self, out, in_, ident, R, W):
        self.add("pe", lambda e: e.transpose(out, in_, ident), R, W)

    def act(self, out, in_, func, R, W, **kw):
        self.add("act", lambda e: e.activation(out=out, in_=in_, func=func, **kw), R, W)

    def tt(self, eng, out, in0, in1, op, R, W):
        self.add(eng, lambda e: e.tensor_tensor(out=out, in0=in0, in1=in1, op=op), R, W)

    def stt(self, out, in0, scalar, in1, op0, op1, R, W):
        self.add("dve", lambda e: e.scalar_tensor_tensor(out=out, in0=in0, scalar=scalar, in1=in1,
                                                          op0=op0, op1=op1), R, W)

    def ts(self, eng, out, in0, s1, s2, op0, op1, R, W):
        self.add(eng, lambda e: e.tensor_scalar(out=out, in0=in0, scalar1=s1, scalar2=s2, op0=op0, op1=op1), R, W)

    def copy(self, eng, out, in_, R, W):
        if eng == "act":
            self.add("act", lambda e: e.activation(out=out, in_=in_, func=AF.Copy), R, W)
        else:
            self.add(eng, lambda e: e.tensor_copy(out=out, in_=in_), R, W)

    def memset(self, eng, ap, val, W):
        self.add(eng, lambda e: e.memset(ap, val), (), W)

    def dma(self, eng, out, in_, R, W, slow=False):
        if DMA_ALL_SP:
            eng = "sp"
        if slow:
            self.add(eng, lambda e: e.dma_start(out=out, in_=in_, allow_slow_non_contiguous=True), R, W, dma=True)
        else:
            self.add(eng, lambda e: e.dma_start(out=out, in_=in_), R, W, dma=True)

    def finalize(self):
        ops = self.ops
        cnt = {}
        for op in ops:
            cnt[op.semkey] = cnt.get(op.semkey, 0) + 1
            op.pos = cnt[op.semkey]
        know = {}
        for op in ops:
            k = know.setdefault(op.eng, {})
            for d in sorted(op.deps, reverse=True):
                dop = ops[d]
                raw = op.deps[d]
                if (not dop.dma) and dop.eng == op.eng and not raw:
                    continue
                if k.get(dop.semkey, 0) >= dop.pos:
                    continue
                op.waits.append(d)
                dop.signal = True
                for kk, vv in dop.vc.items():
                    if k.get(kk, 0) < vv:
                        k[kk] = vv
                if k.get(dop.semkey, 0) < dop.pos:
                    k[dop.semkey] = dop.pos
            op.vc = dict(k)
        cur = self.prog.cur
        last_on_eng = {}
        for i, op in enumerate(ops):
            if not op.dma:
                last_on_eng[op.eng] = i
        for i, op in enumerate(ops):
            if op.dma or last_on_eng.get(op.eng) == i:
                op.signal = True
            if op.signal:
                cur[op.semkey] += 16 if op.dma else 1
                op.semval = cur[op.semkey]
        for op in ops:
            op.vc = None

    def emit(self):
        self.finalize()
        ops = self.ops
        prog = self.prog
        nc = prog.nc
        sems = prog.sems
        per = {}
        for op in ops:
            per.setdefault(op.eng, []).append(op)
        finals = dict(prog.cur)
        prog.n_inst += len(ops)
        prog.phase_no += 1
        phase_no = prog.phase_no

        def make(eng_name, lst):
            def body(e):
                for op in lst:
                    ws = {}
                    for d in op.waits:
                        dop = ops[d]
                        if ws.get(dop.semkey, 0) < dop.semval:
                            ws[dop.semkey] = dop.semval
                    for kk, vv in ws.items():
                        e.wait_ge(sems[kk], vv)
                    inst = op.fn(e)
                    if op.signal:
                        inst.then_inc(sems[op.semkey], 16 if op.dma else 1)
                if eng_name == "sp":
                    for kk, vv in finals.items():
                        if vv > 0:
                            e.wait_ge(sems[kk], vv)
                    for g_ in COMPUTE:
                        e.sem_inc(prog.gates[g_], 1)
                else:
                    e.wait_ge(prog.gates[eng_name], phase_no)
            return body

        if os.environ.get("DBG_DUMP"):
            for i, op in enumerate(ops):
                ws = [(ops[d].semkey, ops[d].semval) for d in op.waits]
                print("OP", i, op.eng, "dma" if op.dma else "", "waits", ws, "sig", (op.semkey, op.semval) if op.signal else None)
            print("FINALS", {k: v for k, v in finals.items() if v})
        for eng_name in ("sp", "pe", "act", "dve", "pool"):
            prog.bodies.setdefault(eng_name, []).append(make(eng_name, per.get(eng_name, [])))


class Rot:
    def __init__(self, alloc, name, n, shape, dt):
        self.bufs = [(alloc(name + str(i), shape, dt), name + str(i)) for i in range(n)]
        self.i = 0

    def next(self):
        b = self.bufs[self.i % len(self.bufs)]
        self.i += 1
        return b


class K:
    def __init__(self, nc, stack, debug=None):
        self.nc = nc
        self.prog = Prog(nc, stack)
        self.debug = debug
        self.ARENA_W = 52400
        self.arena = stack.enter_context(nc.sbuf_tensor("arena", [128, self.ARENA_W], F32))
        self.banks = [stack.enter_context(nc.psum_tensor("bank%d" % i, [128, 512], F32)) for i in range(8)]
        self.arena_off = 0
        self.bank_i = 0
        self._specs = {
            "xT": ([NCH, 128, SEQ], F32), "pT": ([2, 2, 128, EXT], F32),
            "posg0": ([128, 33], I32), "posg1": ([128, 36], I32), "posg2": ([128, 48], I32),
            "norm_mix": ([2, 128, NCH], F32), "a_w_in": ([D, 3104], F32), "a_gate_bias": ([32], F32),
            "a_head_norm": ([D], F32), "a_w_out": ([D, D], F32), "b_w_in": ([D, 9216], F32),
            "b_w_out": ([D, D], F32), "norm_ffn": ([2, 128, NCH], F32), "w_gate": ([2, D, HID], F32),
            "w_up": ([2, D, HID], F32), "w_down": ([2, HID, D], F32), "norm_ple": ([2, 128, NCH], F32),
            "ple_gate": ([2, D, D], F32), "ple_proj": ([2, 256, D], F32), "final_norm": ([128, NCH], F32),
            "c_ident": ([128, 128], BF16), "c_ones": ([128, 128], BF16), "c_triF": ([128, 128], BF16),
            "c_triR": ([128, 128], BF16), "c_mask": ([128, 256], BF16), "c_invf": ([128, 16], F32),
        }
        self.declared = {}

        self._scr_specs = {
            "CRs": ([40, 128, 4 * 129], BF16), "h1T": ([NCH, 128, EXT], F32), "yT": ([NCH, 128, EXT], F32),
            "h2T": ([NCH, 128, OWN], F32), "xn1T": ([NCH, 128, EXT], BF16),
            "O0": ([OWN, 8 * 129], F32), "O1": ([OWN, 8 * 129], F32), "O2": ([OWN, 8 * 129], F32),
            "h3T": ([NCH, 128, OWN], F32), "y2T": ([NCH, 128, OWN], F32),
        }
        self.scr_declared = {}
        self.outT = nc.dram_tensor("outT", [NCH, 128, OWN], F32, kind="ExternalOutput").ap()

    def arena_reset(self):
        self.arena_off = 0
        self.bank_i = 0

    def arena_alloc(self, name, shape, dt):
        assert shape[0] == 128
        esz = 2 if dt == BF16 else 4
        n = 1
        for x in shape[1:]:
            n *= x
        words = (n * esz + 3) // 4
        words = (words + 7) // 8 * 8
        off = self.arena_off
        self.arena_off += words
        assert self.arena_off <= self.ARENA_W, "SBUF arena overflow at %s: %d words" % (name, self.arena_off)
        v = self.arena[:, off:off + words]
        if dt != F32:
            v = v.bitcast(dt)
        v = v[:, 0:n]
        if len(shape) == 3:
            v = v.rearrange("p (a b) -> p a b", b=shape[2])
        elif len(shape) == 4:
            v = v.rearrange("p (a b c) -> p a b c", b=shape[2], c=shape[3])
        return v

    def psum_alloc(self, name, shape, dt):
        b = self.banks[self.bank_i]
        self.bank_i += 1
        assert self.bank_i <= 8, "PSUM banks exhausted at " + name
        v = b[:]
        if dt == BF16:
            v = v.bitcast(BF16)
        return v

    def __getattr__(self, name):
        specs = self.__dict__.get("_specs", {})
        if name in specs:
            if name not in self.declared:
                shp, dt = specs[name]
                self.declared[name] = self.nc.dram_tensor(name, shp, dt, kind="ExternalInput").ap()
            return self.declared[name]
        sspecs = self.__dict__.get("_scr_specs", {})
        if name in sspecs:
            if name not in self.scr_declared:
                shp, dt = sspecs[name]
                kind = "ExternalOutput" if (self.debug and name in self.debug) else "Internal"
                self.scr_declared[name] = self.nc.dram_tensor(name, shp, dt, kind=kind).ap()
            return self.scr_declared[name]
        raise AttributeError(name)

    @property
    def Og(self):
        return [getattr(self, "O%d" % g) for g in range(3)]

    @property
    def posg(self):
        return [getattr(self, "posg%d" % g) for g in range(3)]

    def load_w(self, P, dst, dkey, src2d, nk, c0, ncols, stg, dst_c0=0, engs=("act", "dve", "pool")):
        for k in range(nk):
            t, tk = stg.next()
            P.dma("sp", t[:, 0:ncols], src2d[k * 128:(k + 1) * 128, c0:c0 + ncols], [], [tk])
            P.copy(engs[k % len(engs)], dst[:, k, dst_c0:dst_c0 + ncols], t[:, 0:ncols], [tk], [dkey + "_%d_%d" % (k, dst_c0)])

    def load_vec_pc(self, P, dst, dkey, src1d):
        P.dma("sp", dst, src1d, [], [dkey])

    def fm_norm(self, P, xt, xk, wn, wnk, xn, xnk, N, sq, sqk, ps, psk, rs, rsk, ones, eng2="dve"):
        P.act(sq[:, :, 0:N], xt[:, :, 0:N], AF.Square, [xk], [sqk])
        for c in range(NCH):
            P.mm(ps[:, 0:N], ones, sq[:, c, 0:N], c == 0, c == NCH - 1, [sqk, "ones"], [psk])
        P.act(rs[:, 0:N], ps[:, 0:N], AF.Sqrt, [psk], [rsk], scale=1.0 / D, bias=EPS)
        P.add("dve", lambda e: e.reciprocal(out=rs[:, 0:N], in_=rs[:, 0:N]), [rsk], [rsk])
        for c in range(NCH):
            P.stt(xn[:, c, 0:N], xt[:, c, 0:N], wn[:, c:c + 1], rs[:, 0:N], ALU.mult, ALU.mult,
                  [xk, wnk, rsk], [xnk])

    def phase_mlstm(self, mode):
        nc = self.nc
        P = Phase(self.prog)
        self.arena_reset()
        B = mode == "B"
        with contextlib.ExitStack() as st:
            sb = self.arena_alloc
            psb = self.psum_alloc
            ones = sb("ones", [128, 128], BF16)
            triF = sb("triF", [128, 128], BF16)
            triR = sb("triR", [128, 128], BF16)
            mask = sb("mask", [128, 256], BF16)
            ident = sb("ident", [128, 128], BF16)
            gbias = sb("gbias", [128, 32], F32)
            wn = sb("wn", [128, NCH], F32)
            P.dma("sp", ones[:], self.c_ones, [], ["ones"])
            P.dma("sp", triF[:], self.c_triF, [], ["triF"])
            P.dma("sp", triR[:], self.c_triR, [], ["triR"])
            P.dma("sp", mask[:], self.c_mask, [], ["mask"])
            P.dma("sp", ident[:], self.c_ident, [], ["ident"])
            P.dma("sp", gbias[:], self.a_gate_bias.partition_broadcast(128), [], ["gbias"])
            self.load_vec_pc(P, wn[:], "wn", self.norm_mix[0])
            stg = Rot(sb, "stg", 2, [128, 1552], F32)
            Wb = sb("Wb", [128, NCH, 3104], BF16)
            wkeys = lambda lo, hi: ["Wb_%d_%d" % (k, c) for k in range(NCH) for c in (0, 1552)]
            WK = wkeys(0, 0)
            if B:
                self.load_w(P, Wb, "Wb", self.a_w_in, NCH, 0, 1552, stg, 0)
                self.load_w(P, Wb, "Wb", self.a_w_in, NCH, 1552, 1552, stg, 1552)
                Wout = sb("Wout", [128, NCH, D], BF16)
                self.load_w(P, Wout, "Wout", self.a_w_out, NCH, 0, D, stg, 0)
                WOK = ["Wout_%d_0" % k for k in range(NCH)]
                hnw = sb("hnw", [128, D], F32)
                P.dma("sp", hnw[:], self.a_head_norm.partition_broadcast(128), [], ["hnw"])
            else:
                for k in range(NCH):
                    t, tk = stg.next()
                    P.dma("sp", t[:, 0:1536], self.a_w_in[k * 128:(k + 1) * 128, 512:2048], [], [tk])
                    P.copy(("act", "dve", "pool")[k % 3], Wb[:, k, 512:2048], t[:, 0:1536], [tk], ["Wb_%d_0" % k])
                    t, tk = stg.next()
                    P.dma("sp", t[:, 0:32], self.a_w_in[k * 128:(k + 1) * 128, 3072:3104], [], [tk])
                    P.copy("dve", Wb[:, k, 3072:3104], t[:, 0:32], [tk], ["Wb_%d_1552" % k])
            xts = Rot(sb, "xt", 2, [128, NCH, 512], F32)
            sqs = Rot(sb, "sq", 2, [128, 512], BF16)
            xns = Rot(sb, "xn", 1, [128, NCH, 512], BF16)
            rss = Rot(sb, "rs", 1, [128, 512], F32)
            kks = Rot(sb, "kk", 2, [128, 8, 64], BF16)
            v1s = Rot(sb, "v1", 2, [128, 8, 132], BF16)
            for (t, tk) in v1s.bufs:
                P.memset("pool", t[:, :, 128:129], 1.0, [tk])
            gbs = Rot(sb, "gb", 2, [128, 32], F32)
            e1s = Rot(sb, "e1", 2, [128, 16], F32)
            lfs = Rot(sb, "lf", 2, [128, 16], F32)
            lhs_ = Rot(sb, "lh", 2, [128, 16], BF16)
            lls_ = Rot(sb, "ll", 2, [128, 16], BF16)
            lrs_ = Rot(sb, "lr", 2, [128, 16], F32)
            a1s = Rot(sb, "a1", 2, [128, 16], F32)
            a2s = Rot(sb, "a2", 2, [128, 16], F32)
            EKs = Rot(sb, "EK", 2, [128, 16], F32)
            EKKs = Rot(sb, "EKK", 2, [128, 16], F32)
            EQs = Rot(sb, "EQ", 2, [128, 16], F32)
            EGs = Rot(sb, "EG", 2, [128, 16], F32)
            EGps = Rot(sb, "EGp", 2, [128, 4], F32)
            Cst = sb("Cst", [128, 4, 129], F32)
            P.memset("pool", Cst[:], 0.0, ["Cst"])
            Cbfs = Rot(sb, "Cbf", 2, [128, 4, 129], BF16)
            Ctmp = sb("Ctmp", [128, 4, 129], F32)
            pj = Rot(psb, "pj", 3, [128, 512], F32)
            psD = psb("psD", [128, 512], F32)
            psE = psb("psE", [128, 512], F32) if int(os.environ.get("DBG_X", "0")) == 4 else psD
            if B:
                qTs = Rot(sb, "qT", 1, [128, 4, 512], BF16)
                kTs = Rot(sb, "kT", 1, [128, 4, 512], BF16)
                sigs = Rot(sb, "sig", 2, [128, D], BF16)
                mks = Rot(sb, "mk", 4, [128, 128], BF16)
                NUs = Rot(sb, "NU", 1, [128, 16, 129], F32)
                CRbs = Rot(sb, "CRb", 2, [128, 4, 129], BF16)
                dds = Rot(sb, "dd", 2, [128, 16], F32)
                ffs = Rot(sb, "ff", 2, [128, 16], F32)
                hhs = Rot(sb, "hh", 1, [128, 2, D], F32)
                hss = Rot(sb, "hs", 1, [128, D], F32)
                sss = Rot(sb, "ss", 2, [128, 8], F32)
                hgs = Rot(sb, "hg", 2, [128, D], BF16)
                hgTs = Rot(sb, "hgT", 1, [128, NCH, 512], BF16)
                psS_ = psb("psS", [128, 512], F32)
                psS = psS_[:].rearrange("p (a b) -> p a b", b=128)
                psN = Rot(psb, "psN", 2, [128, 512], F32)
                psT_ = psb("psT", [128, 1024], BF16)
                psT = psT_[:].rearrange("p (a b) -> p a b", b=128)

            st_order = list(range(10)) if B else list(range(15, -1, -1))
            _nst = int(os.environ.get("DBG_NST", "99"))
            _stage = int(os.environ.get("DBG_STAGE", "99"))
            st_order = st_order[:_nst]
            for sti in st_order:
                t0 = sti * 512
                xt, xk = xts.next()
                P.dma("sp", xt[:, 0:4, :], self.xT[0:4, :, t0:t0 + 512].rearrange("c p t -> p c t"), [],
                      [xk + "a"] + [xk + "o%d" % i for i in range(4)])
                P.dma("pool", xt[:, 4:8, :], self.xT[4:8, :, t0:t0 + 512].rearrange("c p t -> p c t"), [],
                      [xk + "b"] + [xk + "o%d" % i for i in range(4, 8)])
                XK = [xk + "a", xk + "b"]
                msps, mspk = pj.next()
                for c in range(NCH):
                    sq, sqk = sqs.next()
                    P.act(sq[:], xt[:, c, :], AF.Square, XK, [sqk])
                    P.mm(msps[:], ones[:], sq[:], c == 0, c == NCH - 1, [sqk, "ones"], [mspk])
                rs, rsk = rss.next()
                P.act(rs[:], msps[:], AF.Sqrt, [mspk], [rsk], scale=1.0 / D, bias=EPS)
                P.add("dve", lambda e, rs=rs: e.reciprocal(out=rs[:], in_=rs[:]), [rsk], [rsk])
                xn, xnk = xns.next()
                for c in range(NCH):
                    P.stt(xn[:, c, :], xt[:, c, :], wn[:, c:c + 1], rs[:], ALU.mult, ALU.mult,
                          XK + ["wn", rsk], [xnk])
                if B:
                    qT, qTk = qTs.next()
                    kT, kTk = kTs.next()
                    for u in range(8):
                        ps, pk = pj.next()
                        for c in range(NCH):
                            P.mm(ps[:], Wb[:, c, u * 128:(u + 1) * 128], xn[:, c, :], c == 0, c == NCH - 1,
                                 WK + [xnk], [pk])
                        if u < 4:
                            P.act(qT[:, u, :], ps[:], AF.Copy, [pk], [qTk], scale=0.125)
                        else:
                            P.copy("dve", kT[:, u - 4, :], ps[:], [pk], [kTk])
                    hgT, hgTk = hgTs.next()
                tiles = range(4) if B else range(3, -1, -1)
                if _stage < 1:
                    continue
                for ti in tiles:
                    t = sti * 4 + ti
                    ts_ = slice(ti * 128, (ti + 1) * 128)
                    kps, kpk = pj.next()
                    for c in range(NCH):
                        P.mm(kps[:], xn[:, c, ts_], Wb[:, c, 512:1024], c == 0, c == NCH - 1, WK + [xnk], [kpk])
                    for c in range(NCH):
                        P.mm(psD[:, 400:432], xn[:, c, ts_], Wb[:, c, 3072:3104], c == 0, c == NCH - 1,
                             WK + [xnk], ["psM"])
                    gb, gbk = gbs.next()
                    P.tt("dve", gb[:], psD[:, 400:432], gbias[:], ALU.add, ["psM", "gbias"], [gbk])
                    e1, e1k = e1s.next()
                    P.act(e1[:], gb[:, 16:32], AF.Exp, [gbk], [e1k], scale=-1.0)
                    lf, lfk = lfs.next()
                    P.act(lf[:], e1[:], AF.Ln, [e1k], [lfk], bias=1.0)
                    if _stage < 2:
                        continue
                    lh, lhk = lhs_.next()
                    ll, llk = lls_.next()
                    lr, lrk = lrs_.next()
                    P.copy("dve", lh[:], lf[:], [lfk], [lhk])
                    P.tt("dve", lr[:], lf[:], lh[:], ALU.subtract, [lfk, lhk], [lrk])
                    P.copy("dve", ll[:], lr[:], [lrk], [llk])
                    for (x_, xk_, a_, b_) in ((lh, lhk, True, False), (ll, llk, False, True)):
                        P.mm(psD[:, 432:440], triF[:], x_[:, 0:8], a_, b_, [xk_, "triF"], ["psM2"])
                    for (x_, xk_, a_, b_) in ((lh, lhk, True, False), (ll, llk, False, True)):
                        P.mm(psD[:, 440:448], triR[:], x_[:, 8:16], a_, b_, [xk_, "triR"], ["psM2"])
                    for (x_, xk_, a_, b_) in ((lh, lhk, True, False), (ll, llk, False, True)):
                        P.mm(psD[:, 448:464], ones[:], x_[:], a_, b_, [xk_, "ones"], ["psM2"])
                    a1, a1k = a1s.next()
                    P.tt("dve", a1[:], psD[:, 432:448], gb[:, 0:16], ALU.add, ["psM2", gbk], [a1k])
                    a2, a2k = a2s.next()
                    P.tt("dve", a2[:], a1[:], psD[:, 448:464], ALU.subtract, ["psM2", a1k], [a2k])
                    EKK, EKKk = EKKs.next()
                    P.act(EKK[:], a2[:], AF.Exp, [a2k], [EKKk])
                    EG, EGk = EGs.next()
                    P.act(EG[:], psD[:, 448:464], AF.Exp, ["psM2"], [EGk], scale=-1.0)
                    dsel = 0 if B else 8
                    EGp, EGpk = EGps.next()
                    P.copy("dve", EGp[0:64, :], EG[0:64, dsel:dsel + 8:2], [EGk], [EGpk])
                    P.copy("dve", EGp[64:128, :], EG[64:128, dsel + 1:dsel + 8:2], [EGk], [EGpk])
                    if B:
                        EK, EKk = EKs.next()
                        P.act(EK[:], a1[:], AF.Exp, [a1k], [EKk])
                        EQ, EQk = EQs.next()
                        P.act(EQ[:], psD[:, 432:448], AF.Exp, ["psM2"], [EQk], scale=-1.0)
                    kk, kkk = kks.next()
                    P.tt("dve", kk[:], kps[:].rearrange("p (h d) -> p h d", d=64),
                         EKK[:, dsel:dsel + 8].unsqueeze(2).to_broadcast([128, 8, 64]), ALU.mult,
                         [kpk, EKKk], [kkk])
                    v1, v1k = v1s.next()
                    for half in range(2):
                        vps, vpk = pj.next()
                        for c in range(NCH):
                            P.mm(vps[:], xn[:, c, ts_], Wb[:, c, 1024 + half * 512:1536 + half * 512],
                                 c == 0, c == NCH - 1, WK + [xnk], [vpk])
                        P.act(v1[:, half * 4:(half + 1) * 4, 0:128], vps[:].rearrange("p (h d) -> p h d", d=128),
                              AF.Copy, [vpk], [v1k])
                    if _stage < 3:
                        continue
                    Cbf, Cbfk = Cbfs.next()
                    P.copy("pool", Cbf[:], Cst[:], ["Cst"], [Cbfk])
                    if not B:
                        if t < 40:
                            P.dma("sp", self.CRs[t], Cbf[:].rearrange("p a b -> p (a b)"), [Cbfk], ["CRs%d" % t])
                    else:
                        CRb, CRbk = CRbs.next()
                        if os.environ.get("DBG_NOCR"):
                            P.copy("pool", CRb[:], Cst[:], ["Cst"], [CRbk])
                        else:
                            P.dma("sp", CRb[:].rearrange("p a b -> p (a b)"), self.CRs[t], [], [CRbk])
                        sig, sigk = sigs.next()
                        for half in range(2):
                            ops_, opk = pj.next()
                            for c in range(NCH):
                                P.mm(ops_[:], xn[:, c, ts_], Wb[:, c, 2048 + half * 512:2560 + half * 512],
                                     c == 0, c == NCH - 1, WK + [xnk], [opk])
                            P.act(sig[:, half * 512:(half + 1) * 512], ops_[:], AF.Sigmoid, [opk], [sigk])
                        NU, NUk = NUs.next()
                        tq = slice(ti * 128, (ti + 1) * 128)
                        for h in range(8):
                            pr, p0 = h // 2, (h % 2) * 64
                            sk = "psS%d" % (h % 4)
                            P.mm(psS[:, h % 4, :], kT[p0:p0 + 64, pr, tq], qT[p0:p0 + 64, pr, tq], True, True,
                                 [kTk, qTk], [sk])
                            pn, pnk = psN.next()
                            for di in range(2):
                                mk, mkk = mks.next()
                                P.stt(mk[:], psS[:, h % 4, :], EK[:, di * 8 + h:di * 8 + h + 1],
                                      mask[:, (1 - di) * 128:(2 - di) * 128], ALU.mult, ALU.mult,
                                      [sk, EKk, "mask"], [mkk])
                                Cm, Cmk = (Cbf, Cbfk) if di == 0 else (CRb, CRbk)
                                P.mm(pn[:, di * 129:(di + 1) * 129], mk[:], v1[:, h, 0:129], True, False, [mkk, v1k], [pnk])
                                P.mm(pn[:, di * 129:(di + 1) * 129], qT[p0:p0 + 64, pr, tq], Cm[p0:p0 + 64, pr, :],
                                     False, True, [qTk, Cmk], [pnk])
                            P.act(NU[:, 2 * h:2 * h + 2, :], pn[:, 0:258].rearrange("p (a b) -> p a b", b=129),
                                  AF.Copy, [pnk], [NUk])
                    if _stage < 4:
                        continue
                    for pr in range(4):
                        slot = pr % 3
                        dk_ = "psD%d" % slot
                        _x = int(os.environ.get("DBG_X", "0"))
                        for hh in range(2):
                            if _x == 1 and hh == 1:
                                continue
                            h = pr * 2 + hh
                            kw = {"tile_position": (0, 64)} if hh == 1 else {}
                            if _x == 3:
                                if hh == 0:
                                    P.mm(psD[:, slot * 129:(slot + 1) * 129], kk[:, h:h + 2, :], v1[:, h, 0:129],
                                         True, True, [kkk, v1k], [dk_])
                                continue
                            P.mm(psE[hh * 64:(hh + 1) * 64, slot * 129:(slot + 1) * 129], kk[:, h, :], v1[:, h, 0:129],
                                 True, True, [kkk, v1k], [dk_], **kw)
                        if _x == 2:
                            continue
                        P.ts("pool", Ctmp[:, pr, :], Cst[:, pr, :], EGp[:, pr:pr + 1], None, ALU.mult, ALU.bypass,
                             ["Cst", EGpk], ["Ctmp%d" % pr])
                        P.tt("dve", Cst[:, pr, :], psE[:, slot * 129:(slot + 1) * 129], Ctmp[:, pr, :], ALU.add,
                             ["Ctmp%d" % pr, dk_], ["Cst"])
                    if not B:
                        continue
                    if _stage < 5:
                        continue
                    dd, ddk = dds.next()
                    P.tt("dve", dd[:].rearrange("p (h d) -> p h d", d=2), NU[:, :, 128].rearrange("p (h d) -> p h d", d=2),
                         EQ[:].rearrange("p (d h) -> p h d", d=2), ALU.mult, [NUk, EQk], [ddk])
                    ff, ffk = ffs.next()
                    P.ts("dve", ff[:], dd[:], -1.0, None, ALU.mult, ALU.bypass, [ddk], [ffk])
                    P.stt(dd[:], dd[:], 1.0, ff[:], ALU.max, ALU.max, [ddk, ffk], [ddk])
                    P.add("dve", lambda e, dd=dd: e.reciprocal(out=dd[:], in_=dd[:]), [ddk], [ddk])
                    P.tt("dve", ff[:].rearrange("p (h d) -> p h d", d=2), dd[:].rearrange("p (h d) -> p h d", d=2),
                         EQ[:].rearrange("p (d h) -> p h d", d=2), ALU.mult, [ddk, EQk], [ffk])
                    hh_, hhk = hhs.next()
                    P.tt("dve", hh_[:].rearrange("p d (h v) -> p h d v", v=128),
                         NU[:, :, 0:128].rearrange("p (h d) v -> p h d v", d=2),
                         ff[:].rearrange("p (h d) -> p h d", d=2).unsqueeze(3).to_broadcast([128, 8, 2, 128]),
                         ALU.mult, [NUk, ffk], [hhk])
                    hs, hsk = hss.next()
                    P.tt("pool", hs[:], hh_[:, 0, :], hh_[:, 1, :], ALU.add, [hhk], [hsk])
                    P.tt("pool", hh_[:, 0, :], hs[:], hs[:], ALU.mult, [hsk], [hhk])
                    ss, ssk = sss.next()
                    P.add("dve", lambda e, ss=ss, hh_=hh_: e.tensor_reduce(
                        out=ss[:], in_=hh_[:, 0, :].rearrange("p (h v) -> p h v", v=128), axis=AX.X, op=ALU.add),
                        [hhk], [ssk])
                    P.act(ss[:], ss[:], AF.Sqrt, [ssk], [ssk], scale=1.0 / 128, bias=EPS)
                    P.add("dve", lambda e, ss=ss: e.reciprocal(out=ss[:], in_=ss[:]), [ssk], [ssk])
                    P.tt("dve", hh_[:, 1, :].rearrange("p (h v) -> p h v", v=128), hs[:].rearrange("p (h v) -> p h v", v=128),
                         ss[:].unsqueeze(2).to_broadcast([128, 8, 128]), ALU.mult, [hsk, ssk], [hhk])
                    P.tt("pool", hh_[:, 0, :], hh_[:, 1, :], hnw[:], ALU.mult, [hhk, "hnw"], [hhk])
                    hg, hgk = hgs.next()
                    P.tt("dve", hg[:], hh_[:, 0, :], sig[:], ALU.mult, [hhk, sigk], [hgk])
                    if _stage < 6:
                        continue
                    for c in range(NCH):
                        P.tr(psT[:, c, :], hg[:, c * 128:(c + 1) * 128], ident[:], [hgk, "ident"], ["psT"])
                    P.copy("act", hgT[:, :, ts_], psT[:], ["psT"], [hgTk + str(ti)])
                if not B or _stage < 7:
                    continue
                HK = [hgTk + str(i) for i in range(4)]
                for oc in range(NCH):
                    ps, pk = pj.next()
                    for c in range(NCH):
                        P.mm(ps[:], Wout[:, c, oc * 128:(oc + 1) * 128], hgT[:, c, :], c == 0, c == NCH - 1,
                             WOK + HK, [pk])
                    P.tt("dve", xt[:, oc, :], xt[:, oc, :], ps[:], ALU.add, XK + [pk], [xk + "o%d" % oc])
                    P.dma("sp" if oc % 2 == 0 else "pool", self.h1T[oc, :, t0:t0 + 512], xt[:, oc, :],
                          [xk + "o%d" % oc], [])
            P.emit()


def _consts():
    s = np.arange(128)[:, None]
    l = np.arange(128)[None, :]
    maskF = (s <= l).astype(np.float32)
    maskR = (s >= l).astype(np.float32)
    invf = (500000.0 ** (-np.arange(0, 32, 2, dtype=np.float32) / 32)).astype(np.float32)
    return {
        "c_ident": np.eye(128, dtype=np.float32).astype(ml_dtypes.bfloat16),
        "c_ones": np.ones((128, 128), np.float32).astype(ml_dtypes.bfloat16),
        "c_triF": maskF.astype(ml_dtypes.bfloat16), "c_triR": maskR.astype(ml_dtypes.bfloat16),
        "c_mask": np.concatenate([maskR, maskF], axis=1).astype(ml_dtypes.bfloat16),
        "c_invf": np.tile(invf[None, :], (128, 1)).astype(np.float32),
    }


def make_in_maps(inputs, cores=range(8)):
    consts = _consts()
    maps = []
    gperm = np.concatenate([np.arange(8, 16), np.arange(0, 8), np.arange(24, 32), np.arange(16, 24)])
    for c in cores:
        b, odd = c // 2, c % 2
        idx = np.arange(SEQ)[::-1] if odd else np.arange(SEQ)
        m = dict(consts)
        m["xT"] = np.ascontiguousarray(inputs["x"][b][idx].T).reshape(NCH, 128, SEQ)
        m["pT"] = np.ascontiguousarray(
            np.stack([inputs["p"][i][b][idx][:EXT].T.reshape(2, 128, EXT) for i in range(2)]))
        pos = np.ascontiguousarray(inputs["positions"][b][idx]).astype(np.int32)
        for g, dd in enumerate((1, 4, 16)):
            NT = OWN // dd // 128
            ii = np.arange(128)[:, None, None]
            rr = np.arange(dd)[None, :, None]
            jj = np.arange(NT + 1)[None, None, :]
            m["posg%d" % g] = pos[(128 * jj + ii) * dd + rr].reshape(128, dd * (NT + 1))
        w_in = inputs["a_w_in"][0]
        gbias = inputs["a_gate_bias"][0]
        if odd:
            w_in = np.concatenate([w_in[:, :3072], w_in[:, 3072:][:, gperm]], axis=1)
            gbias = gbias[gperm]
        m["a_w_in"] = np.ascontiguousarray(w_in)
        m["a_gate_bias"] = np.ascontiguousarray(gbias)
        m["a_head_norm"] = np.ascontiguousarray(inputs["a_head_norm"][0].reshape(D))
        m["a_w_out"] = inputs["a_w_out"][0]
        m["b_w_in"] = inputs["b_w_in"][0]
        m["b_w_out"] = inputs["b_w_out"][0]
        for k in ("w_gate", "w_up", "w_down", "ple_gate", "ple_proj"):
            m[k] = inputs[k]
        for k in ("norm_mix", "norm_ffn", "norm_ple"):
            m[k] = inputs[k].reshape(2, NCH, 128).transpose(0, 2, 1)
        m["final_norm"] = inputs["final_norm"].reshape(NCH, 128).T
        maps.append({k: np.ascontiguousarray(v) for k, v in m.items()})
    return maps


def build(debug=None, phases=None):
    nc = bass.Bass("TRN2", target_bir_lowering=False)
    with contextlib.ExitStack() as stack:
        k = K(nc, stack, debug)
        for ph in (phases or ALL_PHASES):
            args = []
            for a in ph[1:]:
                if isinstance(a, str) and a not in ("A", "B", "bf16", "f32") and (a in k.__dict__ or a in k._specs or a in k._scr_specs):
                    a = getattr(k, a)
                elif isinstance(a, tuple):
                    a = getattr(k, a[0])[a[1]] if len(a) == 2 else getattr(k, a[0])
                elif a == "bf16":
                    a = BF16
                elif a == "f32":
                    a = F32
                args.append(a)
            getattr(k, ph[0])(*args)
        k.prog.emit_all()
    return nc, k


ALL_PHASES = [("phase_mlstm", "A"), ("phase_mlstm", "B")]


def _phase_ffn(self, layer, src, dst, T):
    nc = self.nc
    P = Phase(self.prog)
    self.arena_reset()
    FT = 256
    with contextlib.ExitStack() as st:
        sb = self.arena_alloc
        psb = self.psum_alloc
        ones = sb("ones", [128, 128], BF16)
        wn = sb("wn", [128, NCH], F32)
        P.dma("sp", ones[:], self.c_ones, [], ["ones"])
        self.load_vec_pc(P, wn[:], "wn", self.norm_ffn[layer])
        stg = Rot(sb, "stg", 2, [128, 1408], F32)
        Wg = sb("Wg", [128, NCH, HID], BF16)
        Wu = sb("Wu", [128, NCH, HID], BF16)
        Wd = sb("Wd", [128, NHC, D], BF16)
        for half in range(2):
            self.load_w(P, Wg, "Wg", self.w_gate[layer], NCH, half * 1408, 1408, stg, half * 1408)
            self.load_w(P, Wu, "Wu", self.w_up[layer], NCH, half * 1408, 1408, stg, half * 1408)
        self.load_w(P, Wd, "Wd", self.w_down[layer], NHC, 0, D, stg, 0)
        WGK = ["Wg_%d_%d" % (k, c) for k in range(NCH) for c in (0, 1408)]
        WUK = ["Wu_%d_%d" % (k, c) for k in range(NCH) for c in (0, 1408)]
        WDK = ["Wd_%d_0" % k for k in range(NHC)]
        hts = Rot(sb, "ht", 2, [128, NCH, FT], F32)
        sqs = Rot(sb, "sq", 2, [128, FT], BF16)
        rss = Rot(sb, "rs", 1, [128, FT], F32)
        hns = Rot(sb, "hn", 1, [128, NCH, FT], BF16)
        acs = Rot(sb, "ac", 1, [128, NHC, FT], BF16)
        sis = Rot(sb, "si", 3, [128, FT], F32)
        pn_ = Rot(psb, "pn", 1, [128, 512], F32)
        pg = Rot(psb, "pg", 2, [128, 512], F32)
        pu = Rot(psb, "pu", 2, [128, 512], F32)
        pd = Rot(psb, "pd", 2, [128, 512], F32)
        for ti in range(T // FT):
            t0 = ti * FT
            ht, hk = hts.next()
            P.dma("sp", ht[:, 0:4, :], src[0:4, :, t0:t0 + FT].rearrange("c p t -> p c t"), [],
                  [hk + "a"] + [hk + "o%d" % i for i in range(4)])
            P.dma("pool", ht[:, 4:8, :], src[4:8, :, t0:t0 + FT].rearrange("c p t -> p c t"), [],
                  [hk + "b"] + [hk + "o%d" % i for i in range(4, 8)])
            HK = [hk + "a", hk + "b"]
            msps, mspk = pn_.next()
            for c in range(NCH):
                sq, sqk = sqs.next()
                P.act(sq[:], ht[:, c, :], AF.Square, HK, [sqk])
                P.mm(msps[:, 0:FT], ones[:], sq[:], c == 0, c == NCH - 1, [sqk, "ones"], [mspk])
            rs, rsk = rss.next()
            P.act(rs[:], msps[:, 0:FT], AF.Sqrt, [mspk], [rsk], scale=1.0 / D, bias=EPS)
            P.add("dve", lambda e, rs=rs: e.reciprocal(out=rs[:], in_=rs[:]), [rsk], [rsk])
            hn, hnk = hns.next()
            for c in range(NCH):
                P.stt(hn[:, c, :], ht[:, c, :], wn[:, c:c + 1], rs[:], ALU.mult, ALU.mult, HK + ["wn", rsk], [hnk])
            ac, ack = acs.next()
            for j in range(NHC):
                g, gk = pg.next()
                u, uk = pu.next()
                for c in range(NCH):
                    P.mm(g[:, 0:FT], Wg[:, c, j * 128:(j + 1) * 128], hn[:, c, :], c == 0, c == NCH - 1, WGK + [hnk], [gk])
                for c in range(NCH):
                    P.mm(u[:, 0:FT], Wu[:, c, j * 128:(j + 1) * 128], hn[:, c, :], c == 0, c == NCH - 1, WUK + [hnk], [uk])
                si, sik = sis.next()
                P.act(si[:], g[:, 0:FT], AF.Silu, [gk], [sik])
                P.tt("dve", ac[:, j, :], u[:, 0:FT], si[:], ALU.mult, [uk, sik], [ack + str(j)])
            AK = [ack + str(j) for j in range(NHC)]
            for oc in range(NCH):
                d_, dk = pd.next()
                for j in range(NHC):
                    P.mm(d_[:, 0:FT], Wd[:, j, oc * 128:(oc + 1) * 128], ac[:, j, :], j == 0, j == NHC - 1, WDK + AK, [dk])
                P.tt("dve", ht[:, oc, :], d_[:, 0:FT], ht[:, oc, :], ALU.add, HK + [dk], [hk + "o%d" % oc])
                P.dma("sp" if oc % 2 == 0 else "pool", dst[oc, :, t0:t0 + FT], ht[:, oc, :], [hk + "o%d" % oc], [])
        P.emit()


def _phase_ple(self, layer, src, T, dst_h, Th, dst_n, nw_src, n_dt):
    nc = self.nc
    P = Phase(self.prog)
    self.arena_reset()
    N = 512
    with contextlib.ExitStack() as st:
        sb = self.arena_alloc
        psb = self.psum_alloc
        ones = sb("ones", [128, 128], BF16)
        wn = sb("wn", [128, NCH], F32)
        wn2 = sb("wn2", [128, NCH], F32)
        P.dma("sp", ones[:], self.c_ones, [], ["ones"])
        self.load_vec_pc(P, wn[:], "wn", self.norm_ple[layer])
        self.load_vec_pc(P, wn2[:], "wn2", nw_src)
        stg = Rot(sb, "stg", 2, [128, D], F32)
        Wpg = sb("Wpg", [128, NCH, D], BF16)
        Wpp = sb("Wpp", [128, 2, D], BF16)
        self.load_w(P, Wpg, "Wpg", self.ple_gate[layer], NCH, 0, D, stg, 0)
        self.load_w(P, Wpp, "Wpp", self.ple_proj[layer], 2, 0, D, stg, 0)
        WGK = ["Wpg_%d_0" % k for k in range(NCH)]
        WPK = ["Wpp_%d_0" % k for k in range(2)]
        yts = Rot(sb, "yt", 2, [128, NCH, N], F32)
        pts = Rot(sb, "pt", 2, [128, 2, N], F32)
        pbs = Rot(sb, "pb", 2, [128, 2, N], BF16)
        sqs = Rot(sb, "sq", 2, [128, N], BF16)
        rss = Rot(sb, "rs", 2, [128, N], F32)
        yns = Rot(sb, "yn", 1, [128, NCH, N], BF16)
        sgs = Rot(sb, "sg", 3, [128, N], F32)
        ons = Rot(sb, "on", 2, [128, NCH, N], n_dt)
        pn_ = Rot(psb, "pn", 2, [128, 512], F32)
        pgt = Rot(psb, "pg", 3, [128, 512], F32)
        ppp = Rot(psb, "pp", 3, [128, 512], F32)

        def norm(xt, XK, w, wk, out, outk):
            msps, mspk = pn_.next()
            for c in range(NCH):
                sq, sqk = sqs.next()
                P.act(sq[:], xt[:, c, :], AF.Square, XK, [sqk])
                P.mm(msps[:], ones[:], sq[:], c == 0, c == NCH - 1, [sqk, "ones"], [mspk])
            rs, rsk = rss.next()
            P.act(rs[:], msps[:], AF.Sqrt, [mspk], [rsk], scale=1.0 / D, bias=EPS)
            P.add("dve", lambda e, rs=rs: e.reciprocal(out=rs[:], in_=rs[:]), [rsk], [rsk])
            for c in range(NCH):
                P.stt(out[:, c, :], xt[:, c, :], w[:, c:c + 1], rs[:], ALU.mult, ALU.mult, XK + [wk, rsk], [outk])

        for ti in range(T // N):
            t0 = ti * N
            yt, yk = yts.next()
            okeys = [yk + "o%d" % i for i in range(NCH)]
            P.dma("sp", yt[:, 0:4, :], src[0:4, :, t0:t0 + N].rearrange("c p t -> p c t"), [], [yk + "a"] + okeys[0:4])
            P.dma("pool", yt[:, 4:8, :], src[4:8, :, t0:t0 + N].rearrange("c p t -> p c t"), [], [yk + "b"] + okeys[4:8])
            YK = [yk + "a", yk + "b"]
            pt, ptk = pts.next()
            P.dma("sp", pt[:], self.pT[layer, :, :, t0:t0 + N].rearrange("c p t -> p c t"), [], [ptk])
            pb, pbk = pbs.next()
            P.copy("pool", pb[:], pt[:], [ptk], [pbk])
            yn, ynk = yns.next()
            norm(yt, YK, wn, "wn", yn, ynk)
            for oc in range(NCH):
                g, gk = pgt.next()
                for c in range(NCH):
                    P.mm(g[:], Wpg[:, c, oc * 128:(oc + 1) * 128], yn[:, c, :], c == 0, c == NCH - 1, WGK + [ynk], [gk])
                pp, ppk = ppp.next()
                for c in range(2):
                    P.mm(pp[:], Wpp[:, c, oc * 128:(oc + 1) * 128], pb[:, c, :], c == 0, c == 1, WPK + [pbk], [ppk])
                sg, sgk = sgs.next()
                P.act(sg[:], g[:], AF.Sigmoid, [gk], [sgk])
                P.tt("dve", sg[:], pp[:], sg[:], ALU.mult, [ppk, sgk], [sgk])
                P.tt("pool", yt[:, oc, :], yt[:, oc, :], sg[:], ALU.add, YK + [sgk], [okeys[oc]])
                if dst_h is not None and t0 < Th:
                    P.dma("sp" if oc % 2 == 0 else "pool", dst_h[oc, :, t0:t0 + N], yt[:, oc, :], [okeys[oc]], [])
            on, onk = ons.next()
            norm(yt, okeys, wn2, "wn2", on, onk)
            P.dma("sp", dst_n[0:4, :, t0:t0 + N].rearrange("c p t -> p c t"), on[:, 0:4, :], [onk], [])
            P.dma("pool", dst_n[4:8, :, t0:t0 + N].rearrange("c p t -> p c t"), on[:, 4:8, :], [onk], [])
        P.emit()


K.phase_ffn = _phase_ffn
K.phase_ple = _phase_ple


TWO_PI = 6.283185307179586
C1 = 6.28125
C2 = TWO_PI - 6.28125
MAGIC = 12582912.0
PI = 3.141592653589793
PI_SAFE = 3.1415925


def _phase_attn(self, g):
    nc = self.nc
    P = Phase(self.prog)
    self.arena_reset()
    d = (1, 4, 16)[g]
    U = OWN // d
    NT = U // 128
    ntile = d * (NT + 1)
    col0 = g * 3072
    scale = 128 ** -0.5
    with contextlib.ExitStack() as st:
        sb = self.arena_alloc
        psb = self.psum_alloc
        ident = sb("ident", [128, 128], BF16)
        mask = sb("mask", [128, 256], BF16)
        invf = sb("invf", [128, 16], F32)
        P.dma("sp", ident[:], self.c_ident, [], ["ident"])
        P.dma("sp", mask[:], self.c_mask, [], ["mask"])
        P.dma("sp", invf[:], self.c_invf, [], ["invf"])
        mask3 = mask[:].rearrange("p (a b) -> p a b", b=128)
        posi = sb("posi", [128, ntile], I32)
        for r in range(d):
            P.dma("sp", posi[:, r * (NT + 1):(r + 1) * (NT + 1)], self.posg[g][:, r * (NT + 1):(r + 1) * (NT + 1)],
                  [], ["posi%d" % r])
        PK = ["posi%d" % r for r in range(d)]
        posf = sb("posf", [128, ntile], F32)
        P.copy("dve", posf[:], posi[:], PK, ["posf"])
        ang = sb("ang", [128, ntile, 16], F32)
        P.tt("dve", ang[:], posf[:].unsqueeze(2).to_broadcast([128, ntile, 16]),
             invf[:].unsqueeze(1).to_broadcast([128, ntile, 16]), ALU.mult, ["posf", "invf"], ["ang"])
        tq = sb("tq", [128, ntile, 16], F32)
        rr = sb("rr", [128, ntile, 16], F32)
        P.ts("dve", tq[:], ang[:], 1.0 / TWO_PI, MAGIC, ALU.mult, ALU.add, ["ang"], ["tq"])
        P.ts("dve", tq[:], tq[:], -MAGIC, None, ALU.add, ALU.bypass, ["tq"], ["tq"])
        P.stt(rr[:], tq[:], -C1, ang[:], ALU.mult, ALU.add, ["tq", "ang"], ["rr"])
        P.stt(rr[:], tq[:], -C2, rr[:], ALU.mult, ALU.add, ["tq", "rr"], ["rr"])
        P.ts("dve", rr[:], rr[:], PI_SAFE, -PI_SAFE, ALU.min, ALU.max, ["rr"], ["rr"])
        CS = sb("CS", [128, ntile, 32], F32)
        SN = sb("SN", [128, ntile, 32], F32)
        P.act(SN[:, :, 16:32], rr[:], AF.Sin, ["rr"], ["SN"])
        P.ts("dve", SN[:, :, 0:16], SN[:, :, 16:32], -1.0, None, ALU.mult, ALU.bypass, ["SN"], ["SN"])
        P.ts("dve", tq[:], rr[:], PI / 2, None, ALU.is_gt, ALU.bypass, ["rr"], ["tq"])
        P.stt(rr[:], tq[:], -TWO_PI, rr[:], ALU.mult, ALU.add, ["tq", "rr"], ["rr"])
        P.ts("dve", rr[:], rr[:], PI / 2, None, ALU.add, ALU.bypass, ["rr"], ["rr"])
        P.ts("dve", rr[:], rr[:], PI_SAFE, -PI_SAFE, ALU.min, ALU.max, ["rr"], ["rr"])
        P.act(CS[:, :, 0:16], rr[:], AF.Sin, ["rr"], ["CS"])
        P.copy("dve", CS[:, :, 16:32], CS[:, :, 0:16], ["CS"], ["CS"])
        _stage = int(os.environ.get("DBG_STAGE", "99"))
        xn = sb("xn", [128, NCH, EXT], BF16)
        for c in range(NCH):
            P.dma("sp" if c % 2 == 0 else "pool", xn[:, c, :], self.xn1T[c], [], ["xn%d" % c])
        XK = ["xn%d" % c for c in range(NCH)]
        stg = Rot(sb, "stg", 2, [128, 1024], F32)
        W = sb("W", [128, NCH, 3072], BF16)
        for part in range(3):
            self.load_w(P, W, "W", self.b_w_in, NCH, col0 + part * 1024, 1024, stg, part * 1024)
        WK = ["W_%d_%d" % (k, c) for k in range(NCH) for c in (0, 1024, 2048)]
        qbs = Rot(sb, "qb", 2, [128, D], BF16)
        kbs = Rot(sb, "kb", 2, [128, D], BF16)
        for (t_, tk_) in qbs.bufs + kbs.bufs:
            P.memset("pool", t_[:], 0.0, [tk_ + "m0", tk_ + "m1", tk_ + "r0", tk_ + "r1"])
        tAs = Rot(sb, "tA", 2, [128, 4, 32], F32)
        tBs = Rot(sb, "tB", 2, [128, 4, 32], F32)
        xrs = Rot(sb, "xr", 2, [128, 4, 32], F32)
        QT = sb("QT", [128, 8, 5 * 128], BF16)
        KTs = Rot(sb, "KT", 3, [128, 8, 128], BF16)
        V1s = Rot(sb, "V1", 3, [128, 8, 132], BF16)
        for (t, tk) in V1s.bufs:
            P.memset("pool", t[:, :, 128:129], 1.0, [tk])
        pes = Rot(sb, "pe", 3, [128, 2, 128], BF16)
        PTs = Rot(sb, "PT", 3, [128, 2, 128], BF16)
        Osts = Rot(sb, "Ost", 2, [128, 8, 129], F32)
        pj = Rot(psb, "pj", 3, [128, 512], F32)
        psTs = Rot(psb, "psT", 2, [128, 1024], BF16)
        psSs = Rot(psb, "psS", 2, [128, 512], F32)
        psO_ = psb("psO", [128, 512], F32)
        nmask = 0
        for r in range(d if _stage >= 1 else 0):
            kt_hist = {}
            for j in range(NT + 1):
                cnt = 128 if j < NT else 64
                tcol = r * (NT + 1) + j
                start = 128 * j * d + r
                tsl = slice(start, start + (cnt - 1) * d + 1, d)
                have_q = j < NT
                KT, KTk = KTs.next()
                V1, V1k = V1s.next()
                kt_hist[j] = (KT, KTk, V1, V1k, cnt)
                for which in ((0, 1, 2) if have_q else (1, 2)):
                    if which == 0:
                        dstb, dstk = qbs.next()
                    elif which == 1:
                        dstb, dstk = kbs.next()
                    for half in range(2):
                        ps, pk = pj.next()
                        cb = which * 1024 + half * 512
                        for c in range(NCH):
                            P.mm(ps[0:cnt, :], xn[:, c, tsl], W[:, c, cb:cb + 512], c == 0, c == NCH - 1,
                                 [XK[c]] + WK, [pk])
                        ps3 = ps[0:cnt, :].rearrange("p (h e) -> p h e", e=128)
                        if _stage < 2:
                            continue
                        if which == 2:
                            P.act(V1[0:cnt, half * 4:(half + 1) * 4, 0:128], ps3, AF.Copy, [pk], [V1k])
                            continue
                        d3 = dstb[0:cnt, half * 512:(half + 1) * 512].rearrange("p (h e) -> p h e", e=128)
                        P.act(d3[:, :, 32:128], ps3[:, :, 32:128], AF.Copy, [pk], [dstk + "m%d" % half])
                        tA, tAk = tAs.next()
                        tB, tBk = tBs.next()
                        xr, xrk = xrs.next()
                        P.act(xr[0:cnt], ps3[:, :, 0:32], AF.Copy, [pk], [xrk])
                        pk = xrk
                        ps3 = xr[0:cnt]
                        P.tt("dve", tA[0:cnt], ps3[:, :, 0:32],
                             CS[0:cnt, tcol, :].unsqueeze(1).to_broadcast([cnt, 4, 32]), ALU.mult, [pk, "CS"], [tAk])
                        P.tt("dve", tB[0:cnt, :, 0:16], ps3[:, :, 16:32],
                             SN[0:cnt, tcol, 0:16].unsqueeze(1).to_broadcast([cnt, 4, 16]), ALU.mult, [pk, "SN"], [tBk])
                        P.tt("dve", tB[0:cnt, :, 16:32], ps3[:, :, 0:16],
                             SN[0:cnt, tcol, 16:32].unsqueeze(1).to_broadcast([cnt, 4, 16]), ALU.mult, [pk, "SN"], [tBk])
                        P.tt("dve" if os.environ.get("DBG_X") != "7" else "pool", d3[:, :, 0:32], tA[0:cnt], tB[0:cnt], ALU.add, [tAk, tBk], [dstk + "r%d" % half])
                    if which == 2 or _stage < 3:
                        continue
                    DK = [dstk + "m0", dstk + "m1", dstk + "r0", dstk + "r1"]
                    pT, pTk = psTs.next()
                    pT3 = pT[:].rearrange("p (h e) -> p h e", e=128)
                    for h in range(8):
                        P.tr(pT3[:, h, :], dstb[:, h * 128:(h + 1) * 128], ident[:], DK + ["ident"], [pTk])
                    if which == 0:
                        slot = j % 4
                        P.copy("act", QT[:, :, slot * 128:(slot + 1) * 128], pT3[:, :, :], [pTk], ["QT%d" % slot])
                        if slot == 0:
                            P.copy("pool", QT[:, :, 512:640], QT[:, :, 0:128], ["QT0"], ["QT4"])
                    else:
                        P.copy("act", KT[:], pT3[:, :, :], [pTk], [KTk])
                pieces = []
                if j == 0:
                    pieces.append((-1, 64, 128))
                else:
                    pieces.append((j - 1, 0, 128 if j < NT else 64))
                if _stage < 4:
                    pieces = []
                for (qp, l0, l1) in pieces:
                    nq = l1 - l0
                    u0 = 128 * qp + 64 + l0
                    ring = ((u0 // 128) % 4) * 128 + (u0 % 128)
                    s0, s1 = (u0 // 128) % 4, ((u0 + nq - 1) // 128) % 4
                    QK = ["QT%d" % s0] + (["QT%d" % (4 if s1 == 0 and s0 == 3 else s1)] if s1 != s0 else [])
                    parts = []
                    if qp >= 0:
                        parts.append((0,) + kt_hist[qp])
                    parts.append((1,) + kt_hist[qp + 1])
                    Ost, Ostk = Osts.next()
                    for h in range(8):
                        pS, pSk = psSs.next()
                        pS3 = pS[:, 0:256].rearrange("p (a b) -> p a b", b=128)
                        for (pi, KT_, KTk_, V1_, V1k_, cnt_) in parts:
                            P.mm(pS3[0:cnt_, pi, 0:nq], KT_[:, h, 0:cnt_], QT[:, h, ring:ring + nq], True, True,
                                 [KTk_] + QK, [pSk])
                        pe_, pek = pes.next()
                        PT, PTk = PTs.next()
                        full = len(parts) == 2 and all(p[5] == 128 for p in parts)
                        meng = "pool" if nmask % 3 == 2 else "dve"
                        nmask += 1
                        if full:
                            P.act(pe_[:, :, 0:nq], pS3[:, :, 0:nq], AF.Exp, [pSk], [pek], scale=scale)
                            P.tt(meng, PT[:, :, 0:nq], pe_[:, :, 0:nq], mask3[:, :, l0:l1], ALU.mult, [pek, "mask"], [PTk])
                        else:
                            for (pi, KT_, KTk_, V1_, V1k_, cnt_) in parts:
                                P.act(pe_[0:cnt_, pi, 0:nq], pS3[0:cnt_, pi, 0:nq], AF.Exp, [pSk], [pek], scale=scale)
                                P.tt(meng, PT[0:cnt_, pi, 0:nq], pe_[0:cnt_, pi, 0:nq], mask3[0:cnt_, pi, l0:l1], ALU.mult,
                                     [pek, "mask"], [PTk])
                        ob = (h % 3) * 129
                        ok = "psO%d" % (h % 3)
                        for n_, (pi, KT_, KTk_, V1_, V1k_, cnt_) in enumerate(parts):
                            P.mm(psO_[0:nq, ob:ob + 129], PT[0:cnt_, pi, 0:nq], V1_[0:cnt_, h, 0:129], n_ == 0,
                                 n_ == len(parts) - 1, [PTk, V1k_], [ok])
                        P.copy("act" if h % 2 == 0 else "dve", Ost[0:nq, h, :], psO_[0:nq, ob:ob + 129], [ok], [Ostk + str(h)])
                    if _stage < 5:
                        continue
                    dst = bass.AP(self.Og[g].tensor, (u0 * d + r) * 1032, [[d * 1032, nq], [1, 1032]])
                    P.dma("sp", dst, Ost[0:nq].rearrange("p a b -> p (a b)"), [Ostk + str(h) for h in range(8)],
                          [Ostk + str(h) for h in range(8)] if False else [])
        P.emit()


def _phase_merge(self):
    nc = self.nc
    P = Phase(self.prog)
    self.arena_reset()
    N = 512
    with contextlib.ExitStack() as st:
        sb = self.arena_alloc
        psb = self.psum_alloc
        ident = sb("ident", [128, 128], BF16)
        P.dma("sp", ident[:], self.c_ident, [], ["ident"])
        stg = Rot(sb, "stg", 2, [128, D], F32)
        Wo = sb("Wo", [128, NCH, D], BF16)
        self.load_w(P, Wo, "Wo", self.b_w_out, NCH, 0, D, stg, 0)
        WOK = ["Wo_%d_0" % k for k in range(NCH)]
        Os = [Rot(sb, "O%d_" % g, 2, [128, 8, 129], F32) for g in range(3)]
        rds = Rot(sb, "rd", 2, [128, 8], F32)
        obs = Rot(sb, "ob", 2, [128, D], BF16)
        oTs = Rot(sb, "oT", 2, [128, NCH, N], BF16)
        hts = Rot(sb, "ht", 2, [128, NCH, N], F32)
        psTs = Rot(psb, "psT", 2, [128, 1024], BF16)
        pj = Rot(psb, "pj", 3, [128, 512], F32)
        for sti in range(OWN // N):
            t0 = sti * N
            ht, hk = hts.next()
            okeys = [hk + "o%d" % i for i in range(NCH)]
            P.dma("sp", ht[:, 0:4, :], self.h2T[0:4, :, t0:t0 + N].rearrange("c p t -> p c t"), [], [hk + "a"] + okeys[0:4])
            P.dma("pool", ht[:, 4:8, :], self.h2T[4:8, :, t0:t0 + N].rearrange("c p t -> p c t"), [], [hk + "b"] + okeys[4:8])
            HK = [hk + "a", hk + "b"]
            oT, oTk = oTs.next()
            for ti in range(4):
                tt0 = t0 + ti * 128
                tl = []
                for g in range(3):
                    o_, ok_ = Os[g].next()
                    P.dma("sp" if g != 1 else "pool", o_[:].rearrange("p a b -> p (a b)"), self.Og[g][tt0:tt0 + 128, :], [], [ok_])
                    tl.append((o_, ok_))
                (o0, k0), (o1, k1), (o2, k2) = tl
                P.tt("pool", o0[:], o0[:], o1[:], ALU.add, [k0, k1], [k0])
                P.tt("dve", o0[:], o0[:], o2[:], ALU.add, [k0, k2], [k0])
                rd, rdk = rds.next()
                P.add("dve", lambda e, rd=rd, o0=o0: e.reciprocal(out=rd[:], in_=o0[:, :, 128]), [k0], [rdk])
                ob, obk = obs.next()
                P.tt("dve", ob[:].rearrange("p (h e) -> p h e", e=128), o0[:, :, 0:128],
                     rd[:].unsqueeze(2).to_broadcast([128, 8, 128]), ALU.mult, [k0, rdk], [obk])
                pT, pTk = psTs.next()
                pT3 = pT[:].rearrange("p (h e) -> p h e", e=128)
                for h in range(8):
                    P.tr(pT3[:, h, :], ob[:, h * 128:(h + 1) * 128], ident[:], [obk, "ident"], [pTk])
                P.copy("act", oT[:, :, ti * 128:(ti + 1) * 128], pT3, [pTk], [oTk + str(ti)])
            OK_ = [oTk + str(i) for i in range(4)]
            for oc in range(NCH):
                ps, pk = pj.next()
                for c in range(NCH):
                    P.mm(ps[:], Wo[:, c, oc * 128:(oc + 1) * 128], oT[:, c, :], c == 0, c == NCH - 1, WOK + OK_, [pk])
                P.tt("dve", ht[:, oc, :], ps[:], ht[:, oc, :], ALU.add, HK + [pk], [okeys[oc]])
                P.dma("sp" if oc % 2 == 0 else "pool", self.h3T[oc, :, t0:t0 + N], ht[:, oc, :], [okeys[oc]], [])
        P.emit()


K.phase_attn = _phase_attn
K.phase_merge = _phase_merge

CORES_PER_LAUNCH = 2

ALL_PHASES = [
    ("phase_mlstm", "A"), ("phase_mlstm", "B"),
    ("phase_ffn", 0, "h1T", "yT", EXT),
    ("phase_ple", 0, "yT", EXT, "h2T", OWN, "xn1T", ("norm_mix", 1), "bf16"),
    ("phase_attn", 0), ("phase_attn", 1), ("phase_attn", 2),
    ("phase_merge",),
    ("phase_ffn", 1, "h3T", "y2T", OWN),
    ("phase_ple", 1, "y2T", OWN, None, 0, "outT", ("final_norm",), "f32"),
]


def kernel(**inputs):
    inputs = {k: np.asarray(v) for k, v in inputs.items()}
    nc, k_ = build()
    maps = [{n: v for n, v in m.items() if n in k_.declared} for m in make_in_maps(inputs, range(8))]
    results = []
    for i in range(0, 8, CORES_PER_LAUNCH):
        res = run_bass_kernel_spmd(nc, maps[i:i + CORES_PER_LAUNCH], core_ids=list(range(CORES_PER_LAUNCH)))
        results.extend(res.results)
    out = np.empty((4, SEQ, D), np.float32)
    for c in range(8):
        b, odd = c // 2, c % 2
        o = np.asarray(results[c]["outT"]).reshape(D, OWN).T
        if odd:
            out[b, SEQ - 1 - np.arange(OWN)] = o
        else:
            out[b, :OWN] = o
    return out
```

```python
import contextlib
import os
import numpy as np
import ml_dtypes
import concourse.bass as bass
import concourse.mybir as mybir
from concourse.bass_utils import run_bass_kernel_spmd

F32 = mybir.dt.float32
BF16 = mybir.dt.bfloat16
I32 = mybir.dt.int32
AF = mybir.ActivationFunctionType
ALU = mybir.AluOpType
AX = mybir.AxisListType

D = 1024
SEQ = 8192
OWN = 4096
EXT = 5120
NCH = 8
HID = 2816
NHC = 22
EPS = 1e-6
COMPUTE = ("pe", "act", "dve", "pool")
N_DMA_SEMS = int(os.environ.get("N_DMA_SEMS", "32"))
DMA_ALL_SP = os.environ.get('DMA_ALL_SP', '1') == '1'


BANK_ALIAS = {"psM": "bkD", "psM2": "bkD", "psD0": "bkD", "psD1": "bkD", "psD2": "bkD",
              "psS0": "bkS", "psS1": "bkS", "psS2": "bkS", "psS3": "bkS",
              "psO0": "bkO", "psO1": "bkO", "psO2": "bkO"}


class Op:
    __slots__ = ("eng", "fn", "deps", "dma", "pos", "semkey", "semval", "waits", "signal", "vc")

    def __init__(self, eng, fn, dma):
        self.eng = eng
        self.fn = fn
        self.dma = dma
        self.deps = {}
        self.waits = []
        self.signal = False
        self.vc = None


class Prog:
    def __init__(self, nc, stack):
        self.nc = nc
        self.sems = {}
        for i_ in range(int(os.environ.get("SEM_SKIP", "0"))):
            stack.enter_context(nc.semaphore("skip%d" % i_))
        for e in COMPUTE:
            self.sems[e] = stack.enter_context(nc.semaphore("s_" + e))
        for s in range(N_DMA_SEMS):
            self.sems[("d", s)] = stack.enter_context(nc.semaphore("s_d%d" % s))
        self.cur = {k: 0 for k in self.sems}
        self.gates = {e: stack.enter_context(nc.semaphore("g_" + e)) for e in COMPUTE}
        self.phase_no = 0
        self.ndma = 0
        self.n_inst = 0
        self.bodies = {}

    def emit_all(self):
        nc = self.nc
        bodies = self.bodies
        with nc.Block() as block:
            deco = {"pe": block.tensor, "act": block.scalar, "dve": block.vector,
                    "pool": block.gpsimd, "sp": block.sync}
            for eng_name in ("sp", "pe", "act", "dve", "pool"):
                def run(e, lst=bodies.get(eng_name, [])):
                    for f in lst:
                        f(e)
                deco[eng_name](run)


class Phase:
    def __init__(self, prog):
        self.prog = prog
        self.ops = []
        self.lastw = {}
        self.readers = {}
        self.dma_last = {}

    def add(self, eng, fn, reads=(), writes=(), dma=False):
        reads = list(reads) + [BANK_ALIAS[r] for r in reads if r in BANK_ALIAS]
        writes = list(writes) + [BANK_ALIAS[w] for w in writes if w in BANK_ALIAS]
        op = Op(eng, fn, dma)
        idx = len(self.ops)
        for r in reads:
            w = self.lastw.get(r)
            if w is not None:
                op.deps[w] = True
        for wkey in writes:
            w = self.lastw.get(wkey)
            if w is not None and w not in op.deps:
                op.deps[w] = False
            for rd in self.readers.get(wkey, ()):
                if rd not in op.deps:
                    op.deps[rd] = False
        if dma:
            s = self.prog.ndma % N_DMA_SEMS
            self.prog.ndma += 1
            prev = self.dma_last.get(s)
            if prev is not None:
                op.deps[prev] = True
            self.dma_last[s] = idx
            op.semkey = ("d", s)
        else:
            op.semkey = eng
        self.ops.append(op)
        for r in reads:
            self.readers.setdefault(r, []).append(idx)
        for wkey in writes:
            self.lastw[wkey] = idx
            self.readers[wkey] = []
        return idx

    def mm(self, out, lhsT, rhs, start, stop, R, W, **kw):
        self.add("pe", lambda e: e.matmul(out, lhsT=lhsT, rhs=rhs, start=start, stop=stop, **kw), R, W)

    def tr(self, out, in_, ident, R, W):
        self.add("pe", lambda e: e.transpose(out, in_, ident), R, W)

    def act(self, out, in_, func, R, W, **kw):
        self.add("act", lambda e: e.activation(out=out, in_=in_, func=func, **kw), R, W)

    def tt(self, eng, out, in0, in1, op, R, W):
        self.add(eng, lambda e: e.tensor_tensor(out=out, in0=in0, in1=in1, op=op), R, W)

    def stt(self, out, in0, scalar, in1, op0, op1, R, W):
        self.add("dve", lambda e: e.scalar_tensor_tensor(out=out, in0=in0, scalar=scalar, in1=in1,
                                                          op0=op0, op1=op1), R, W)

    def ts(self, eng, out, in0, s1, s2, op0, op1, R, W):
        self.add(eng, lambda e: e.tensor_scalar(out=out, in0=in0, scalar1=s1, scalar2=s2, op0=op0, op1=op1), R, W)

    def copy(self, eng, out, in_, R, W):
        if eng == "act":
            self.add("act", lambda e: e.activation(out=out, in_=in_, func=AF.Copy), R, W)
        else:
            self.add(eng, lambda e: e.tensor_copy(out=out, in_=in_), R, W)

    def memset(self, eng, ap, val, W):
        self.add(eng, lambda e: e.memset(ap, val), (), W)

    def dma(self, eng, out, in_, R, W, slow=False):
        if DMA_ALL_SP:
            eng = "sp"
        if slow:
            self.add(eng, lambda e: e.dma_start(out=out, in_=in_, allow_slow_non_contiguous=True), R, W, dma=True)
        else:
            self.add(eng, lambda e: e.dma_start(out=out, in_=in_), R, W, dma=True)

    def finalize(self):
        ops = self.ops
        cnt = {}
        for op in ops:
            cnt[op.semkey] = cnt.get(op.semkey, 0) + 1
            op.pos = cnt[op.semkey]
        know = {}
        for op in ops:
            k = know.setdefault(op.eng, {})
            for d in sorted(op.deps, reverse=True):
                dop = ops[d]
                raw = op.deps[d]
                if (not dop.dma) and dop.eng == op.eng and not raw:
                    continue
                if k.get(dop.semkey, 0) >= dop.pos:
                    continue
                op.waits.append(d)
                dop.signal = True
                for kk, vv in dop.vc.items():
                    if k.get(kk, 0) < vv:
                        k[kk] = vv
                if k.get(dop.semkey, 0) < dop.pos:
                    k[dop.semkey] = dop.pos
            op.vc = dict(k)
        cur = self.prog.cur
        last_on_eng = {}
        for i, op in enumerate(ops):
            if not op.dma:
                last_on_eng[op.eng] = i
        for i, op in enumerate(ops):
            if op.dma or last_on_eng.get(op.eng) == i:
                op.signal = True
            if op.signal:
                cur[op.semkey] += 16 if op.dma else 1
                op.semval = cur[op.semkey]
        for op in ops:
            op.vc = None

    def emit(self):
        self.finalize()
        ops = self.ops
        prog = self.prog
        nc = prog.nc
        sems = prog.sems
        per = {}
        for op in ops:
            per.setdefault(op.eng, []).append(op)
        finals = dict(prog.cur)
        prog.n_inst += len(ops)
        prog.phase_no += 1
        phase_no = prog.phase_no

        def make(eng_name, lst):
            def body(e):
                for op in lst:
                    ws = {}
                    for d in op.waits:
                        dop = ops[d]
                        if ws.get(dop.semkey, 0) < dop.semval:
                            ws[dop.semkey] = dop.semval
                    for kk, vv in ws.items():
                        e.wait_ge(sems[kk], vv)
                    inst = op.fn(e)
                    if op.signal:
                        inst.then_inc(sems[op.semkey], 16 if op.dma else 1)
                if eng_name == "sp":
                    for kk, vv in finals.items():
                        if vv > 0:
                            e.wait_ge(sems[kk], vv)
                    for g_ in COMPUTE:
                        e.sem_inc(prog.gates[g_], 1)
                else:
                    e.wait_ge(prog.gates[eng_name], phase_no)
            return body

        if os.environ.get("DBG_DUMP"):
            for i, op in enumerate(ops):
                ws = [(ops[d].semkey, ops[d].semval) for d in op.waits]
                print("OP", i, op.eng, "dma" if op.dma else "", "waits", ws, "sig", (op.semkey, op.semval) if op.signal else None)
            print("FINALS", {k: v for k, v in finals.items() if v})
        for eng_name in ("sp", "pe", "act", "dve", "pool"):
            prog.bodies.setdefault(eng_name, []).append(make(eng_name, per.get(eng_name, [])))


class Rot:
    def __init__(self, alloc, name, n, shape, dt):
        self.bufs = [(alloc(name + str(i), shape, dt), name + str(i)) for i in range(n)]
        self.i = 0

    def next(self):
        b = self.bufs[self.i % len(self.bufs)]
        self.i += 1
        return b


class K:
    def __init__(self, nc, stack, debug=None):
        self.nc = nc
        self.prog = Prog(nc, stack)
        self.debug = debug
        self.ARENA_W = 52400
        self.arena = stack.enter_context(nc.sbuf_tensor("arena", [128, self.ARENA_W], F32))
        self.banks = [stack.enter_context(nc.psum_tensor("bank%d" % i, [128, 512], F32)) for i in range(8)]
        self.arena_off = 0
        self.bank_i = 0
        self._specs = {
            "xT": ([NCH, 128, SEQ], F32), "pT": ([2, 2, 128, EXT], F32),
            "posg0": ([128, 33], I32), "posg1": ([128, 36], I32), "posg2": ([128, 48], I32),
            "norm_mix": ([2, 128, NCH], F32), "a_w_in": ([D, 3104], F32), "a_gate_bias": ([32], F32),
            "a_head_norm": ([D], F32), "a_w_out": ([D, D], F32), "b_w_in": ([D, 9216], F32),
            "b_w_out": ([D, D], F32), "norm_ffn": ([2, 128, NCH], F32), "w_gate": ([2, D, HID], F32),
            "w_up": ([2, D, HID], F32), "w_down": ([2, HID, D], F32), "norm_ple": ([2, 128, NCH], F32),
            "ple_gate": ([2, D, D], F32), "ple_proj": ([2, 256, D], F32), "final_norm": ([128, NCH], F32),
            "c_ident": ([128, 128], BF16), "c_ones": ([128, 128], BF16), "c_triF": ([128, 128], BF16),
            "c_triR": ([128, 128], BF16), "c_mask": ([128, 256], BF16), "c_invf": ([128, 16], F32),
        }
        self.declared = {}

        self._scr_specs = {
            "CRs": ([40, 128, 4 * 129], BF16), "h1T": ([NCH, 128, EXT], F32), "yT": ([NCH, 128, EXT], F32),
            "h2T": ([NCH, 128, OWN], F32), "xn1T": ([NCH, 128, EXT], BF16),
            "O0": ([OWN, 8 * 129], F32), "O1": ([OWN, 8 * 129], F32), "O2": ([OWN, 8 * 129], F32),
            "h3T": ([NCH, 128, OWN], F32), "y2T": ([NCH, 128, OWN], F32),
        }
        self.scr_declared = {}
        self.outT = nc.dram_tensor("outT", [NCH, 128, OWN], F32, kind="ExternalOutput").ap()

    def arena_reset(self):
        self.arena_off = 0
        self.bank_i = 0

    def arena_alloc(self, name, shape, dt):
        assert shape[0] == 128
        esz = 2 if dt == BF16 else 4
        n = 1
        for x in shape[1:]:
            n *= x
        words = (n * esz + 3) // 4
        words = (words + 7) // 8 * 8
        off = self.arena_off
        self.arena_off += words
        assert self.arena_off <= self.ARENA_W, "SBUF arena overflow at %s: %d words" % (name, self.arena_off)
        v = self.arena[:, off:off + words]
        if dt != F32:
            v = v.bitcast(dt)
        v = v[:, 0:n]
        if len(shape) == 3:
            v = v.rearrange("p (a b) -> p a b", b=shape[2])
        elif len(shape) == 4:
            v = v.rearrange("p (a b c) -> p a b c", b=shape[2], c=shape[3])
        return v

    def psum_alloc(self, name, shape, dt):
        b = self.banks[self.bank_i]
        self.bank_i += 1
        assert self.bank_i <= 8, "PSUM banks exhausted at " + name
        v = b[:]
        if dt == BF16:
            v = v.bitcast(BF16)
        return v

    def __getattr__(self, name):
        specs = self.__dict__.get("_specs", {})
        if name in specs:
            if name not in self.declared:
                shp, dt = specs[name]
                self.declared[name] = self.nc.dram_tensor(name, shp, dt, kind="ExternalInput").ap()
            return self.declared[name]
        sspecs = self.__dict__.get("_scr_specs", {})
        if name in sspecs:
            if name not in self.scr_declared:
                shp, dt = sspecs[name]
                kind = "ExternalOutput" if (self.debug and name in self.debug) else "Internal"
                self.scr_declared[name] = self.nc.dram_tensor(name, shp, dt, kind=kind).ap()
            return self.scr_declared[name]
        raise AttributeError(name)

    @property
    def Og(self):
        return [getattr(self, "O%d" % g) for g in range(3)]

    @property
    def posg(self):
        return [getattr(self, "posg%d" % g) for g in range(3)]

    def load_w(self, P, dst, dkey, src2d, nk, c0, ncols, stg, dst_c0=0, engs=("act", "dve", "pool")):
        for k in range(nk):
            t, tk = stg.next()
            P.dma("sp", t[:, 0:ncols], src2d[k * 128:(k + 1) * 128, c0:c0 + ncols], [], [tk])
            P.copy(engs[k % len(engs)], dst[:, k, dst_c0:dst_c0 + ncols], t[:, 0:ncols], [tk], [dkey + "_%d_%d" % (k, dst_c0)])

    def load_vec_pc(self, P, dst, dkey, src1d):
        P.dma("sp", dst, src1d, [], [dkey])

    def fm_norm(self, P, xt, xk, wn, wnk, xn, xnk, N, sq, sqk, ps, psk, rs, rsk, ones, eng2="dve"):
        P.act(sq[:, :, 0:N], xt[:, :, 0:N], AF.Square, [xk], [sqk])
        for c in range(NCH):
            P.mm(ps[:, 0:N], ones, sq[:, c, 0:N], c == 0, c == NCH - 1, [sqk, "ones"], [psk])
        P.act(rs[:, 0:N], ps[:, 0:N], AF.Sqrt, [psk], [rsk], scale=1.0 / D, bias=EPS)
        P.add("dve", lambda e: e.reciprocal(out=rs[:, 0:N], in_=rs[:, 0:N]), [rsk], [rsk])
        for c in range(NCH):
            P.stt(xn[:, c, 0:N], xt[:, c, 0:N], wn[:, c:c + 1], rs[:, 0:N], ALU.mult, ALU.mult,
                  [xk, wnk, rsk], [xnk])

    def phase_mlstm(self, mode):
        nc = self.nc
        P = Phase(self.prog)
        self.arena_reset()
        B = mode == "B"
        with contextlib.ExitStack() as st:
            sb = self.arena_alloc
            psb = self.psum_alloc
            ones = sb("ones", [128, 128], BF16)
            triF = sb("triF", [128, 128], BF16)
            triR = sb("triR", [128, 128], BF16)
            mask = sb("mask", [128, 256], BF16)
            ident = sb("ident", [128, 128], BF16)
            gbias = sb("gbias", [128, 32], F32)
            wn = sb("wn", [128, NCH], F32)
            P.dma("sp", ones[:], self.c_ones, [], ["ones"])
            P.dma("sp", triF[:], self.c_triF, [], ["triF"])
            P.dma("sp", triR[:], self.c_triR, [], ["triR"])
            P.dma("sp", mask[:], self.c_mask, [], ["mask"])
            P.dma("sp", ident[:], self.c_ident, [], ["ident"])
            P.dma("sp", gbias[:], self.a_gate_bias.partition_broadcast(128), [], ["gbias"])
            self.load_vec_pc(P, wn[:], "wn", self.norm_mix[0])
            stg = Rot(sb, "stg", 2, [128, 1552], F32)
            Wb = sb("Wb", [128, NCH, 3104], BF16)
            wkeys = lambda lo, hi: ["Wb_%d_%d" % (k, c) for k in range(NCH) for c in (0, 1552)]
            WK = wkeys(0, 0)
            if B:
                self.load_w(P, Wb, "Wb", self.a_w_in, NCH, 0, 1552, stg, 0)
                self.load_w(P, Wb, "Wb", self.a_w_in, NCH, 1552, 1552, stg, 1552)
                Wout = sb("Wout", [128, NCH, D], BF16)
                self.load_w(P, Wout, "Wout", self.a_w_out, NCH, 0, D, stg, 0)
                WOK = ["Wout_%d_0" % k for k in range(NCH)]
                hnw = sb("hnw", [128, D], F32)
                P.dma("sp", hnw[:], self.a_head_norm.partition_broadcast(128), [], ["hnw"])
            else:
                for k in range(NCH):
                    t, tk = stg.next()
                    P.dma("sp", t[:, 0:1536], self.a_w_in[k * 128:(k + 1) * 128, 512:2048], [], [tk])
                    P.copy(("act", "dve", "pool")[k % 3], Wb[:, k, 512:2048], t[:, 0:1536], [tk], ["Wb_%d_0" % k])
                    t, tk = stg.next()
                    P.dma("sp", t[:, 0:32], self.a_w_in[k * 128:(k + 1) * 128, 3072:3104], [], [tk])
                    P.copy("dve", Wb[:, k, 3072:3104], t[:, 0:32], [tk], ["Wb_%d_1552" % k])
            xts = Rot(sb, "xt", 2, [128, NCH, 512], F32)
            sqs = Rot(sb, "sq", 2, [128, 512], BF16)
            xns = Rot(sb, "xn", 1, [128, NCH, 512], BF16)
            rss = Rot(sb, "rs", 1, [128, 512], F32)
            kks = Rot(sb, "kk", 2, [128, 8, 64], BF16)
            v1s = Rot(sb, "v1", 2, [128, 8, 132], BF16)
            for (t, tk) in v1s.bufs:
                P.memset("pool", t[:, :, 128:129], 1.0, [tk])
            gbs = Rot(sb, "gb", 2, [128, 32], F32)
            e1s = Rot(sb, "e1", 2, [128, 16], F32)
            lfs = Rot(sb, "lf", 2, [128, 16], F32)
            lhs_ = Rot(sb, "lh", 2, [128, 16], BF16)
            lls_ = Rot(sb, "ll", 2, [128, 16], BF16)
            lrs_ = Rot(sb, "lr", 2, [128, 16], F32)
            a1s = Rot(sb, "a1", 2, [128, 16], F32)
            a2s = Rot(sb, "a2", 2, [128, 16], F32)
            EKs = Rot(sb, "EK", 2, [128, 16], F32)
            EKKs = Rot(sb, "EKK", 2, [128, 16], F32)
            EQs = Rot(sb, "EQ", 2, [128, 16], F32)
            EGs = Rot(sb, "EG", 2, [128, 16], F32)
            EGps = Rot(sb, "EGp", 2, [128, 4], F32)
            Cst = sb("Cst", [128, 4, 129], F32)
            P.memset("pool", Cst[:], 0.0, ["Cst"])
            Cbfs = Rot(sb, "Cbf", 2, [128, 4, 129], BF16)
            Ctmp = sb("Ctmp", [128, 4, 129], F32)
            pj = Rot(psb, "pj", 3, [128, 512], F32)
            psD = psb("psD", [128, 512], F32)
            psE = psb("psE", [128, 512], F32) if int(os.environ.get("DBG_X", "0")) == 4 else psD
            if B:
                qTs = Rot(sb, "qT", 1, [128, 4, 512], BF16)
                kTs = Rot(sb, "kT", 1, [128, 4, 512], BF16)
                sigs = Rot(sb, "sig", 2, [128, D], BF16)
                mks = Rot(sb, "mk", 4, [128, 128], BF16)
                NUs = Rot(sb, "NU", 1, [128, 16, 129], F32)
                CRbs = Rot(sb, "CRb", 2, [128, 4, 129], BF16)
                dds = Rot(sb, "dd", 2, [128, 16], F32)
                ffs = Rot(sb, "ff", 2, [128, 16], F32)
                hhs = Rot(sb, "hh", 1, [128, 2, D], F32)
                hss = Rot(sb, "hs", 1, [128, D], F32)
                sss = Rot(sb, "ss", 2, [128, 8], F32)
                hgs = Rot(sb, "hg", 2, [128, D], BF16)
                hgTs = Rot(sb, "hgT", 1, [128, NCH, 512], BF16)
                psS_ = psb("psS", [128, 512], F32)
                psS = psS_[:].rearrange("p (a b) -> p a b", b=128)
                psN = Rot(psb, "psN", 2, [128, 512], F32)
                psT_ = psb("psT", [128, 1024], BF16)
                psT = psT_[:].rearrange("p (a b) -> p a b", b=128)

            st_order = list(range(10)) if B else list(range(15, -1, -1))
            _nst = int(os.environ.get("DBG_NST", "99"))
            _stage = int(os.environ.get("DBG_STAGE", "99"))
            st_order = st_order[:_nst]
            for sti in st_order:
                t0 = sti * 512
                xt, xk = xts.next()
                P.dma("sp", xt[:, 0:4, :], self.xT[0:4, :, t0:t0 + 512].rearrange("c p t -> p c t"), [],
                      [xk + "a"] + [xk + "o%d" % i for i in range(4)])
                P.dma("pool", xt[:, 4:8, :], self.xT[4:8, :, t0:t0 + 512].rearrange("c p t -> p c t"), [],
                      [xk + "b"] + [xk + "o%d" % i for i in range(4, 8)])
                XK = [xk + "a", xk + "b"]
                msps, mspk = pj.next()
                for c in range(NCH):
                    sq, sqk = sqs.next()
                    P.act(sq[:], xt[:, c, :], AF.Square, XK, [sqk])
                    P.mm(msps[:], ones[:], sq[:], c == 0, c == NCH - 1, [sqk, "ones"], [mspk])
                rs, rsk = rss.next()
                P.act(rs[:], msps[:], AF.Sqrt, [mspk], [rsk], scale=1.0 / D, bias=EPS)
                P.add("dve", lambda e, rs=rs: e.reciprocal(out=rs[:], in_=rs[:]), [rsk], [rsk])
                xn, xnk = xns.next()
                for c in range(NCH):
                    P.stt(xn[:, c, :], xt[:, c, :], wn[:, c:c + 1], rs[:], ALU.mult, ALU.mult,
                          XK + ["wn", rsk], [xnk])
                if B:
                    qT, qTk = qTs.next()
                    kT, kTk = kTs.next()
                    for u in range(8):
                        ps, pk = pj.next()
                        for c in range(NCH):
                            P.mm(ps[:], Wb[:, c, u * 128:(u + 1) * 128], xn[:, c, :], c == 0, c == NCH - 1,
                                 WK + [xnk], [pk])
                        if u < 4:
                            P.act(qT[:, u, :], ps[:], AF.Copy, [pk], [qTk], scale=0.125)
                        else:
                            P.copy("dve", kT[:, u - 4, :], ps[:], [pk], [kTk])
                    hgT, hgTk = hgTs.next()
                tiles = range(4) if B else range(3, -1, -1)
                if _stage < 1:
                    continue
                for ti in tiles:
                    t = sti * 4 + ti
                    ts_ = slice(ti * 128, (ti + 1) * 128)
                    kps, kpk = pj.next()
                    for c in range(NCH):
                        P.mm(kps[:], xn[:, c, ts_], Wb[:, c, 512:1024], c == 0, c == NCH - 1, WK + [xnk], [kpk])
                    for c in range(NCH):
                        P.mm(psD[:, 400:432], xn[:, c, ts_], Wb[:, c, 3072:3104], c == 0, c == NCH - 1,
                             WK + [xnk], ["psM"])
                    gb, gbk = gbs.next()
                    P.tt("dve", gb[:], psD[:, 400:432], gbias[:], ALU.add, ["psM", "gbias"], [gbk])
                    e1, e1k = e1s.next()
                    P.act(e1[:], gb[:, 16:32], AF.Exp, [gbk], [e1k], scale=-1.0)
                    lf, lfk = lfs.next()
                    P.act(lf[:], e1[:], AF.Ln, [e1k], [lfk], bias=1.0)
                    if _stage < 2:
                        continue
                    lh, lhk = lhs_.next()
                    ll, llk = lls_.next()
                    lr, lrk = lrs_.next()
                    P.copy("dve", lh[:], lf[:], [lfk], [lhk])
                    P.tt("dve", lr[:], lf[:], lh[:], ALU.subtract, [lfk, lhk], [lrk])
                    P.copy("dve", ll[:], lr[:], [lrk], [llk])
                    for (x_, xk_, a_, b_) in ((lh, lhk, True, False), (ll, llk, False, True)):
                        P.mm(psD[:, 432:440], triF[:], x_[:, 0:8], a_, b_, [xk_, "triF"], ["psM2"])
                    for (x_, xk_, a_, b_) in ((lh, lhk, True, False), (ll, llk, False, True)):
                        P.mm(psD[:, 440:448], triR[:], x_[:, 8:16], a_, b_, [xk_, "triR"], ["psM2"])
                    for (x_, xk_, a_, b_) in ((lh, lhk, True, False), (ll, llk, False, True)):
                        P.mm(psD[:, 448:464], ones[:], x_[:], a_, b_, [xk_, "ones"], ["psM2"])
                    a1, a1k = a1s.next()
                    P.tt("dve", a1[:], psD[:, 432:448], gb[:, 0:16], ALU.add, ["psM2", gbk], [a1k])
                    a2, a2k = a2s.next()
                    P.tt("dve", a2[:], a1[:], psD[:, 448:464], ALU.subtract, ["psM2", a1k], [a2k])
                    EKK, EKKk = EKKs.next()
                    P.act(EKK[:], a2[:], AF.Exp, [a2k], [EKKk])
                    EG, EGk = EGs.next()
                    P.act(EG[:], psD[:, 448:464], AF.Exp, ["psM2"], [EGk], scale=-1.0)
                    dsel = 0 if B else 8
                    EGp, EGpk = EGps.next()
                    P.copy("dve", EGp[0:64, :], EG[0:64, dsel:dsel + 8:2], [EGk], [EGpk])
                    P.copy("dve", EGp[64:128, :], EG[64:128, dsel + 1:dsel + 8:2], [EGk], [EGpk])
                    if B:
                        EK, EKk = EKs.next()
                        P.act(EK[:], a1[:], AF.Exp, [a1k], [EKk])
                        EQ, EQk = EQs.next()
                        P.act(EQ[:], psD[:, 432:448], AF.Exp, ["psM2"], [EQk], scale=-1.0)
                    kk, kkk = kks.next()
                    P.tt("dve", kk[:], kps[:].rearrange("p (h d) -> p h d", d=64),
                         EKK[:, dsel:dsel + 8].unsqueeze(2).to_broadcast([128, 8, 64]), ALU.mult,
                         [kpk, EKKk], [kkk])
                    v1, v1k = v1s.next()
                    for half in range(2):
                        vps, vpk = pj.next()
                        for c in range(NCH):
                            P.mm(vps[:], xn[:, c, ts_], Wb[:, c, 1024 + half * 512:1536 + half * 512],
                                 c == 0, c == NCH - 1, WK + [xnk], [vpk])
                        P.act(v1[:, half * 4:(half + 1) * 4, 0:128], vps[:].rearrange("p (h d) -> p h d", d=128),
                              AF.Copy, [vpk], [v1k])
                    if _stage < 3:
                        continue
                    Cbf, Cbfk = Cbfs.next()
                    P.copy("pool", Cbf[:], Cst[:], ["Cst"], [Cbfk])
                    if not B:
                        if t < 40:
                            P.dma("sp", self.CRs[t], Cbf[:].rearrange("p a b -> p (a b)"), [Cbfk], ["CRs%d" % t])
                    else:
                        CRb, CRbk = CRbs.next()
                        if os.environ.get("DBG_NOCR"):
                            P.copy("pool", CRb[:], Cst[:], ["Cst"], [CRbk])
                        else:
                            P.dma("sp", CRb[:].rearrange("p a b -> p (a b)"), self.CRs[t], [], [CRbk])
                        sig, sigk = sigs.next()
                        for half in range(2):
                            ops_, opk = pj.next()
                            for c in range(NCH):
                                P.mm(ops_[:], xn[:, c, ts_], Wb[:, c, 2048 + half * 512:2560 + half * 512],
                                     c == 0, c == NCH - 1, WK + [xnk], [opk])
                            P.act(sig[:, half * 512:(half + 1) * 512], ops_[:], AF.Sigmoid, [opk], [sigk])
                        NU, NUk = NUs.next()
                        tq = slice(ti * 128, (ti + 1) * 128)
                        for h in range(8):
                            pr, p0 = h // 2, (h % 2) * 64
                            sk = "psS%d" % (h % 4)
                            P.mm(psS[:, h % 4, :], kT[p0:p0 + 64, pr, tq], qT[p0:p0 + 64, pr, tq], True, True,
                                 [kTk, qTk], [sk])
                            pn, pnk = psN.next()
                            for di in range(2):
                                mk, mkk = mks.next()
                                P.stt(mk[:], psS[:, h % 4, :], EK[:, di * 8 + h:di * 8 + h + 1],
                                      mask[:, (1 - di) * 128:(2 - di) * 128], ALU.mult, ALU.mult,
                                      [sk, EKk, "mask"], [mkk])
                                Cm, Cmk = (Cbf, Cbfk) if di == 0 else (CRb, CRbk)
                                P.mm(pn[:, di * 129:(di + 1) * 129], mk[:], v1[:, h, 0:129], True, False, [mkk, v1k], [pnk])
                                P.mm(pn[:, di * 129:(di + 1) * 129], qT[p0:p0 + 64, pr, tq], Cm[p0:p0 + 64, pr, :],
                                     False, True, [qTk, Cmk], [pnk])
                            P.act(NU[:, 2 * h:2 * h + 2, :], pn[:, 0:258].rearrange("p (a b) -> p a b", b=129),
                                  AF.Copy, [pnk], [NUk])
                    if _stage < 4:
                        continue
                    for pr in range(4):
                        slot = pr % 3
                        dk_ = "psD%d" % slot
                        _x = int(os.environ.get("DBG_X", "0"))
                        for hh in range(2):
                            if _x == 1 and hh == 1:
                                continue
                            h = pr * 2 + hh
                            kw = {"tile_position": (0, 64)} if hh == 1 else {}
                            if _x == 3:
                                if hh == 0:
                                    P.mm(psD[:, slot * 129:(slot + 1) * 129], kk[:, h:h + 2, :], v1[:, h, 0:129],
                                         True, True, [kkk, v1k], [dk_])
                                continue
                            P.mm(psE[hh * 64:(hh + 1) * 64, slot * 129:(slot + 1) * 129], kk[:, h, :], v1[:, h, 0:129],
                                 True, True, [kkk, v1k], [dk_], **kw)
                        if _x == 2:
                            continue
                        P.ts("pool", Ctmp[:, pr, :], Cst[:, pr, :], EGp[:, pr:pr + 1], None, ALU.mult, ALU.bypass,
                             ["Cst", EGpk], ["Ctmp%d" % pr])
                        P.tt("dve", Cst[:, pr, :], psE[:, slot * 129:(slot + 1) * 129], Ctmp[:, pr, :], ALU.add,
                             ["Ctmp%d" % pr, dk_], ["Cst"])
                    if not B:
                        continue
                    if _stage < 5:
                        continue
                    dd, ddk = dds.next()
                    P.tt("dve", dd[:].rearrange("p (h d) -> p h d", d=2), NU[:, :, 128].rearrange("p (h d) -> p h d", d=2),
                         EQ[:].rearrange("p (d h) -> p h d", d=2), ALU.mult, [NUk, EQk], [ddk])
                    ff, ffk = ffs.next()
                    P.ts("dve", ff[:], dd[:], -1.0, None, ALU.mult, ALU.bypass, [ddk], [ffk])
                    P.stt(dd[:], dd[:], 1.0, ff[:], ALU.max, ALU.max, [ddk, ffk], [ddk])
                    P.add("dve", lambda e, dd=dd: e.reciprocal(out=dd[:], in_=dd[:]), [ddk], [ddk])
                    P.tt("dve", ff[:].rearrange("p (h d) -> p h d", d=2), dd[:].rearrange("p (h d) -> p h d", d=2),
                         EQ[:].rearrange("p (d h) -> p h d", d=2), ALU.mult, [ddk, EQk], [ffk])
                    hh_, hhk = hhs.next()
                    P.tt("dve", hh_[:].rearrange("p d (h v) -> p h d v", v=128),
                         NU[:, :, 0:128].rearrange("p (h d) v -> p h d v", d=2),
                         ff[:].rearrange("p (h d) -> p h d", d=2).unsqueeze(3).to_broadcast([128, 8, 2, 128]),
                         ALU.mult, [NUk, ffk], [hhk])
                    hs, hsk = hss.next()
                    P.tt("pool", hs[:], hh_[:, 0, :], hh_[:, 1, :], ALU.add, [hhk], [hsk])
                    P.tt("pool", hh_[:, 0, :], hs[:], hs[:], ALU.mult, [hsk], [hhk])
                    ss, ssk = sss.next()
                    P.add("dve", lambda e, ss=ss, hh_=hh_: e.tensor_reduce(
                        out=ss[:], in_=hh_[:, 0, :].rearrange("p (h v) -> p h v", v=128), axis=AX.X, op=ALU.add),
                        [hhk], [ssk])
                    P.act(ss[:], ss[:], AF.Sqrt, [ssk], [ssk], scale=1.0 / 128, bias=EPS)
                    P.add("dve", lambda e, ss=ss: e.reciprocal(out=ss[:], in_=ss[:]), [ssk], [ssk])
                    P.tt("dve", hh_[:, 1, :].rearrange("p (h v) -> p h v", v=128), hs[:].rearrange("p (h v) -> p h v", v=128),
                         ss[:].unsqueeze(2).to_broadcast([128, 8, 128]), ALU.mult, [hsk, ssk], [hhk])
                    P.tt("pool", hh_[:, 0, :], hh_[:, 1, :], hnw[:], ALU.mult, [hhk, "hnw"], [hhk])
                    hg, hgk = hgs.next()
                    P.tt("dve", hg[:], hh_[:, 0, :], sig[:], ALU.mult, [hhk, sigk], [hgk])
                    if _stage < 6:
                        continue
                    for c in range(NCH):
                        P.tr(psT[:, c, :], hg[:, c * 128:(c + 1) * 128], ident[:], [hgk, "ident"], ["psT"])
                    P.copy("act", hgT[:, :, ts_], psT[:], ["psT"], [hgTk + str(ti)])
                if not B or _stage < 7:
                    continue
                HK = [hgTk + str(i) for i in range(4)]
                for oc in range(NCH):
                    ps, pk = pj.next()
                    for c in range(NCH):
                        P.mm(ps[:], Wout[:, c, oc * 128:(oc + 1) * 128], hgT[:, c, :], c == 0, c == NCH - 1,
                             WOK + HK, [pk])
                    P.tt("dve", xt[:, oc, :], xt[:, oc, :], ps[:], ALU.add, XK + [pk], [xk + "o%d" % oc])
                    P.dma("sp" if oc % 2 == 0 else "pool", self.h1T[oc, :, t0:t0 + 512], xt[:, oc, :],
                          [xk + "o%d" % oc], [])
            P.emit()


def _consts():
    s = np.arange(128)[:, None]
    l = np.arange(128)[None, :]
    maskF = (s <= l).astype(np.float32)
    maskR = (s >= l).astype(np.float32)
    invf = (500000.0 ** (-np.arange(0, 32, 2, dtype=np.float32) / 32)).astype(np.float32)
    return {
        "c_ident": np.eye(128, dtype=np.float32).astype(ml_dtypes.bfloat16),
        "c_ones": np.ones((128, 128), np.float32).astype(ml_dtypes.bfloat16),
        "c_triF": maskF.astype(ml_dtypes.bfloat16), "c_triR": maskR.astype(ml_dtypes.bfloat16),
        "c_mask": np.concatenate([maskR, maskF], axis=1).astype(ml_dtypes.bfloat16),
        "c_invf": np.tile(invf[None, :], (128, 1)).astype(np.float32),
    }


def make_in_maps(inputs, cores=range(8)):
    consts = _consts()
    maps = []
    gperm = np.concatenate([np.arange(8, 16), np.arange(0, 8), np.arange(24, 32), np.arange(16, 24)])
    for c in cores:
        b, odd = c // 2, c % 2
        idx = np.arange(SEQ)[::-1] if odd else np.arange(SEQ)
        m = dict(consts)
        m["xT"] = np.ascontiguousarray(inputs["x"][b][idx].T).reshape(NCH, 128, SEQ)
        m["pT"] = np.ascontiguousarray(
            np.stack([inputs["p"][i][b][idx][:EXT].T.reshape(2, 128, EXT) for i in range(2)]))
        pos = np.ascontiguousarray(inputs["positions"][b][idx]).astype(np.int32)
        for g, dd in enumerate((1, 4, 16)):
            NT = OWN // dd // 128
            ii = np.arange(128)[:, None, None]
            rr = np.arange(dd)[None, :, None]
            jj = np.arange(NT + 1)[None, None, :]
            m["posg%d" % g] = pos[(128 * jj + ii) * dd + rr].reshape(128, dd * (NT + 1))
        w_in = inputs["a_w_in"][0]
        gbias = inputs["a_gate_bias"][0]
        if odd:
            w_in = np.concatenate([w_in[:, :3072], w_in[:, 3072:][:, gperm]], axis=1)
            gbias = gbias[gperm]
        m["a_w_in"] = np.ascontiguousarray(w_in)
        m["a_gate_bias"] = np.ascontiguousarray(gbias)
        m["a_head_norm"] = np.ascontiguousarray(inputs["a_head_norm"][0].reshape(D))
        m["a_w_out"] = inputs["a_w_out"][0]
        m["b_w_in"] = inputs["b_w_in"][0]
        m["b_w_out"] = inputs["b_w_out"][0]
        for k in ("w_gate", "w_up", "w_down", "ple_gate", "ple_proj"):
            m[k] = inputs[k]
        for k in ("norm_mix", "norm_ffn", "norm_ple"):
            m[k] = inputs[k].reshape(2, NCH, 128).transpose(0, 2, 1)
        m["final_norm"] = inputs["final_norm"].reshape(NCH, 128).T
        maps.append({k: np.ascontiguousarray(v) for k, v in m.items()})
    return maps


def build(debug=None, phases=None):
    nc = bass.Bass("TRN2", target_bir_lowering=False)
    with contextlib.ExitStack() as stack:
        k = K(nc, stack, debug)
        for ph in (phases or ALL_PHASES):
            args = []
            for a in ph[1:]:
                if isinstance(a, str) and a not in ("A", "B", "bf16", "f32") and (a in k.__dict__ or a in k._specs or a in k._scr_specs):
                    a = getattr(k, a)
                elif isinstance(a, tuple):
                    a = getattr(k, a[0])[a[1]] if len(a) == 2 else getattr(k, a[0])
                elif a == "bf16":
                    a = BF16
                elif a == "f32":
                    a = F32
                args.append(a)
            getattr(k, ph[0])(*args)
        k.prog.emit_all()
    return nc, k


ALL_PHASES = [("phase_mlstm", "A"), ("phase_mlstm", "B")]


def _phase_ffn(self, layer, src, dst, T):
    nc = self.nc
    P = Phase(self.prog)
    self.arena_reset()
    FT = 256
    with contextlib.ExitStack() as st:
        sb = self.arena_alloc
        psb = self.psum_alloc
        ones = sb("ones", [128, 128], BF16)
        wn = sb("wn", [128, NCH], F32)
        P.dma("sp", ones[:], self.c_ones, [], ["ones"])
        self.load_vec_pc(P, wn[:], "wn", self.norm_ffn[layer])
        stg = Rot(sb, "stg", 2, [128, 1408], F32)
        Wg = sb("Wg", [128, NCH, HID], BF16)
        Wu = sb("Wu", [128, NCH, HID], BF16)
        Wd = sb("Wd", [128, NHC, D], BF16)
        for half in range(2):
            self.load_w(P, Wg, "Wg", self.w_gate[layer], NCH, half * 1408, 1408, stg, half * 1408)
            self.load_w(P, Wu, "Wu", self.w_up[layer], NCH, half * 1408, 1408, stg, half * 1408)
        self.load_w(P, Wd, "Wd", self.w_down[layer], NHC, 0, D, stg, 0)
        WGK = ["Wg_%d_%d" % (k, c) for k in range(NCH) for c in (0, 1408)]
        WUK = ["Wu_%d_%d" % (k, c) for k in range(NCH) for c in (0, 1408)]
        WDK = ["Wd_%d_0" % k for k in range(NHC)]
        hts = Rot(sb, "ht", 2, [128, NCH, FT], F32)
        sqs = Rot(sb, "sq", 2, [128, FT], BF16)
        rss = Rot(sb, "rs", 1, [128, FT], F32)
        hns = Rot(sb, "hn", 1, [128, NCH, FT], BF16)
        acs = Rot(sb, "ac", 1, [128, NHC, FT], BF16)
        sis = Rot(sb, "si", 3, [128, FT], F32)
        pn_ = Rot(psb, "pn", 1, [128, 512], F32)
        pg = Rot(psb, "pg", 2, [128, 512], F32)
        pu = Rot(psb, "pu", 2, [128, 512], F32)
        pd = Rot(psb, "pd", 2, [128, 512], F32)
        for ti in range(T // FT):
            t0 = ti * FT
            ht, hk = hts.next()
            P.dma("sp", ht[:, 0:4, :], src[0:4, :, t0:t0 + FT].rearrange("c p t -> p c t"), [],
                  [hk + "a"] + [hk + "o%d" % i for i in range(4)])
            P.dma("pool", ht[:, 4:8, :], src[4:8, :, t0:t0 + FT].rearrange("c p t -> p c t"), [],
                  [hk + "b"] + [hk + "o%d" % i for i in range(4, 8)])
            HK = [hk + "a", hk + "b"]
            msps, mspk = pn_.next()
            for c in range(NCH):
                sq, sqk = sqs.next()
                P.act(sq[:], ht[:, c, :], AF.Square, HK, [sqk])
                P.mm(msps[:, 0:FT], ones[:], sq[:], c == 0, c == NCH - 1, [sqk, "ones"], [mspk])
            rs, rsk = rss.next()
            P.act(rs[:], msps[:, 0:FT], AF.Sqrt, [mspk], [rsk], scale=1.0 / D, bias=EPS)
            P.add("dve", lambda e, rs=rs: e.reciprocal(out=rs[:], in_=rs[:]), [rsk], [rsk])
            hn, hnk = hns.next()
            for c in range(NCH):
                P.stt(hn[:, c, :], ht[:, c, :], wn[:, c:c + 1], rs[:], ALU.mult, ALU.mult, HK + ["wn", rsk], [hnk])
            ac, ack = acs.next()
            for j in range(NHC):
                g, gk = pg.next()
                u, uk = pu.next()
                for c in range(NCH):
                    P.mm(g[:, 0:FT], Wg[:, c, j * 128:(j + 1) * 128], hn[:, c, :], c == 0, c == NCH - 1, WGK + [hnk], [gk])
                for c in range(NCH):
                    P.mm(u[:, 0:FT], Wu[:, c, j * 128:(j + 1) * 128], hn[:, c, :], c == 0, c == NCH - 1, WUK + [hnk], [uk])
                si, sik = sis.next()
                P.act(si[:], g[:, 0:FT], AF.Silu, [gk], [sik])
                P.tt("dve", ac[:, j, :], u[:, 0:FT], si[:], ALU.mult, [uk, sik], [ack + str(j)])
            AK = [ack + str(j) for j in range(NHC)]
            for oc in range(NCH):
                d_, dk = pd.next()
                for j in range(NHC):
                    P.mm(d_[:, 0:FT], Wd[:, j, oc * 128:(oc + 1) * 128], ac[:, j, :], j == 0, j == NHC - 1, WDK + AK, [dk])
                P.tt("dve", ht[:, oc, :], d_[:, 0:FT], ht[:, oc, :], ALU.add, HK + [dk], [hk + "o%d" % oc])
                P.dma("sp" if oc % 2 == 0 else "pool", dst[oc, :, t0:t0 + FT], ht[:, oc, :], [hk + "o%d" % oc], [])
        P.emit()


def _phase_ple(self, layer, src, T, dst_h, Th, dst_n, nw_src, n_dt):
    nc = self.nc
    P = Phase(self.prog)
    self.arena_reset()
    N = 512
    with contextlib.ExitStack() as st:
        sb = self.arena_alloc
        psb = self.psum_alloc
        ones = sb("ones", [128, 128], BF16)
        wn = sb("wn", [128, NCH], F32)
        wn2 = sb("wn2", [128, NCH], F32)
        P.dma("sp", ones[:], self.c_ones, [], ["ones"])
        self.load_vec_pc(P, wn[:], "wn", self.norm_ple[layer])
        self.load_vec_pc(P, wn2[:], "wn2", nw_src)
        stg = Rot(sb, "stg", 2, [128, D], F32)
        Wpg = sb("Wpg", [128, NCH, D], BF16)
        Wpp = sb("Wpp", [128, 2, D], BF16)
        self.load_w(P, Wpg, "Wpg", self.ple_gate[layer], NCH, 0, D, stg, 0)
        self.load_w(P, Wpp, "Wpp", self.ple_proj[layer], 2, 0, D, stg, 0)
        WGK = ["Wpg_%d_0" % k for k in range(NCH)]
        WPK = ["Wpp_%d_0" % k for k in range(2)]
        yts = Rot(sb, "yt", 2, [128, NCH, N], F32)
        pts = Rot(sb, "pt", 2, [128, 2, N], F32)
        pbs = Rot(sb, "pb", 2, [128, 2, N], BF16)
        sqs = Rot(sb, "sq", 2, [128, N], BF16)
        rss = Rot(sb, "rs", 2, [128, N], F32)
        yns = Rot(sb, "yn", 1, [128, NCH, N], BF16)
        sgs = Rot(sb, "sg", 3, [128, N], F32)
        ons = Rot(sb, "on", 2, [128, NCH, N], n_dt)
        pn_ = Rot(psb, "pn", 2, [128, 512], F32)
        pgt = Rot(psb, "pg", 3, [128, 512], F32)
        ppp = Rot(psb, "pp", 3, [128, 512], F32)

        def norm(xt, XK, w, wk, out, outk):
            msps, mspk = pn_.next()
            for c in range(NCH):
                sq, sqk = sqs.next()
                P.act(sq[:], xt[:, c, :], AF.Square, XK, [sqk])
                P.mm(msps[:], ones[:], sq[:], c == 0, c == NCH - 1, [sqk, "ones"], [mspk])
            rs, rsk = rss.next()
            P.act(rs[:], msps[:], AF.Sqrt, [mspk], [rsk], scale=1.0 / D, bias=EPS)
            P.add("dve", lambda e, rs=rs: e.reciprocal(out=rs[:], in_=rs[:]), [rsk], [rsk])
            for c in range(NCH):
                P.stt(out[:, c, :], xt[:, c, :], w[:, c:c + 1], rs[:], ALU.mult, ALU.mult, XK + [wk, rsk], [outk])

        for ti in range(T // N):
            t0 = ti * N
            yt, yk = yts.next()
            okeys = [yk + "o%d" % i for i in range(NCH)]
            P.dma("sp", yt[:, 0:4, :], src[0:4, :, t0:t0 + N].rearrange("c p t -> p c t"), [], [yk + "a"] + okeys[0:4])
            P.dma("pool", yt[:, 4:8, :], src[4:8, :, t0:t0 + N].rearrange("c p t -> p c t"), [], [yk + "b"] + okeys[4:8])
            YK = [yk + "a", yk + "b"]
            pt, ptk = pts.next()
            P.dma("sp", pt[:], self.pT[layer, :, :, t0:t0 + N].rearrange("c p t -> p c t"), [], [ptk])
            pb, pbk = pbs.next()
            P.copy("pool", pb[:], pt[:], [ptk], [pbk])
            yn, ynk = yns.next()
            norm(yt, YK, wn, "wn", yn, ynk)
            for oc in range(NCH):
                g, gk = pgt.next()
                for c in range(NCH):
                    P.mm(g[:], Wpg[:, c, oc * 128:(oc + 1) * 128], yn[:, c, :], c == 0, c == NCH - 1, WGK + [ynk], [gk])
                pp, ppk = ppp.next()
                for c in range(2):
                    P.mm(pp[:], Wpp[:, c, oc * 128:(oc + 1) * 128], pb[:, c, :], c == 0, c == 1, WPK + [pbk], [ppk])
                sg, sgk = sgs.next()
                P.act(sg[:], g[:], AF.Sigmoid, [gk], [sgk])
                P.tt("dve", sg[:], pp[:], sg[:], ALU.mult, [ppk, sgk], [sgk])
                P.tt("pool", yt[:, oc, :], yt[:, oc, :], sg[:], ALU.add, YK + [sgk], [okeys[oc]])
                if dst_h is not None and t0 < Th:
                    P.dma("sp" if oc % 2 == 0 else "pool", dst_h[oc, :, t0:t0 + N], yt[:, oc, :], [okeys[oc]], [])
            on, onk = ons.next()
            norm(yt, okeys, wn2, "wn2", on, onk)
            P.dma("sp", dst_n[0:4, :, t0:t0 + N].rearrange("c p t -> p c t"), on[:, 0:4, :], [onk], [])
            P.dma("pool", dst_n[4:8, :, t0:t0 + N].rearrange("c p t -> p c t"), on[:, 4:8, :], [onk], [])
        P.emit()


K.phase_ffn = _phase_ffn
K.phase_ple = _phase_ple


TWO_PI = 6.283185307179586
C1 = 6.28125
C2 = TWO_PI - 6.28125
MAGIC = 12582912.0
PI = 3.141592653589793
PI_SAFE = 3.1415925


def _phase_attn(self, g):
    nc = self.nc
    P = Phase(self.prog)
    self.arena_reset()
    d = (1, 4, 16)[g]
    U = OWN // d
    NT = U // 128
    ntile = d * (NT + 1)
    col0 = g * 3072
    scale = 128 ** -0.5
    with contextlib.ExitStack() as st:
        sb = self.arena_alloc
        psb = self.psum_alloc
        ident = sb("ident", [128, 128], BF16)
        mask = sb("mask", [128, 256], BF16)
        invf = sb("invf", [128, 16], F32)
        P.dma("sp", ident[:], self.c_ident, [], ["ident"])
        P.dma("sp", mask[:], self.c_mask, [], ["mask"])
        P.dma("sp", invf[:], self.c_invf, [], ["invf"])
        mask3 = mask[:].rearrange("p (a b) -> p a b", b=128)
        posi = sb("posi", [128, ntile], I32)
        for r in range(d):
            P.dma("sp", posi[:, r * (NT + 1):(r + 1) * (NT + 1)], self.posg[g][:, r * (NT + 1):(r + 1) * (NT + 1)],
                  [], ["posi%d" % r])
        PK = ["posi%d" % r for r in range(d)]
        posf = sb("posf", [128, ntile], F32)
        P.copy("dve", posf[:], posi[:], PK, ["posf"])
        ang = sb("ang", [128, ntile, 16], F32)
        P.tt("dve", ang[:], posf[:].unsqueeze(2).to_broadcast([128, ntile, 16]),
             invf[:].unsqueeze(1).to_broadcast([128, ntile, 16]), ALU.mult, ["posf", "invf"], ["ang"])
        tq = sb("tq", [128, ntile, 16], F32)
        rr = sb("rr", [128, ntile, 16], F32)
        P.ts("dve", tq[:], ang[:], 1.0 / TWO_PI, MAGIC, ALU.mult, ALU.add, ["ang"], ["tq"])
        P.ts("dve", tq[:], tq[:], -MAGIC, None, ALU.add, ALU.bypass, ["tq"], ["tq"])
        P.stt(rr[:], tq[:], -C1, ang[:], ALU.mult, ALU.add, ["tq", "ang"], ["rr"])
        P.stt(rr[:], tq[:], -C2, rr[:], ALU.mult, ALU.add, ["tq", "rr"], ["rr"])
        P.ts("dve", rr[:], rr[:], PI_SAFE, -PI_SAFE, ALU.min, ALU.max, ["rr"], ["rr"])
        CS = sb("CS", [128, ntile, 32], F32)
        SN = sb("SN", [128, ntile, 32], F32)
        P.act(SN[:, :, 16:32], rr[:], AF.Sin, ["rr"], ["SN"])
        P.ts("dve", SN[:, :, 0:16], SN[:, :, 16:32], -1.0, None, ALU.mult, ALU.bypass, ["SN"], ["SN"])
        P.ts("dve", tq[:], rr[:], PI / 2, None, ALU.is_gt, ALU.bypass, ["rr"], ["tq"])
        P.stt(rr[:], tq[:], -TWO_PI, rr[:], ALU.mult, ALU.add, ["tq", "rr"], ["rr"])
        P.ts("dve", rr[:], rr[:], PI / 2, None, ALU.add, ALU.bypass, ["rr"], ["rr"])
        P.ts("dve", rr[:], rr[:], PI_SAFE, -PI_SAFE, ALU.min, ALU.max, ["rr"], ["rr"])
        P.act(CS[:, :, 0:16], rr[:], AF.Sin, ["rr"], ["CS"])
        P.copy("dve", CS[:, :, 16:32], CS[:, :, 0:16], ["CS"], ["CS"])
        _stage = int(os.environ.get("DBG_STAGE", "99"))
        xn = sb("xn", [128, NCH, EXT], BF16)
        for c in range(NCH):
            P.dma("sp" if c % 2 == 0 else "pool", xn[:, c, :], self.xn1T[c], [], ["xn%d" % c])
        XK = ["xn%d" % c for c in range(NCH)]
        stg = Rot(sb, "stg", 2, [128, 1024], F32)
        W = sb("W", [128, NCH, 3072], BF16)
        for part in range(3):
            self.load_w(P, W, "W", self.b_w_in, NCH, col0 + part * 1024, 1024, stg, part * 1024)
        WK = ["W_%d_%d" % (k, c) for k in range(NCH) for c in (0, 1024, 2048)]
        qbs = Rot(sb, "qb", 2, [128, D], BF16)
        kbs = Rot(sb, "kb", 2, [128, D], BF16)
        for (t_, tk_) in qbs.bufs + kbs.bufs:
            P.memset("pool", t_[:], 0.0, [tk_ + "m0", tk_ + "m1", tk_ + "r0", tk_ + "r1"])
        tAs = Rot(sb, "tA", 2, [128, 4, 32], F32)
        tBs = Rot(sb, "tB", 2, [128, 4, 32], F32)
        xrs = Rot(sb, "xr", 2, [128, 4, 32], F32)
        QT = sb("QT", [128, 8, 5 * 128], BF16)
        KTs = Rot(sb, "KT", 3, [128, 8, 128], BF16)
        V1s = Rot(sb, "V1", 3, [128, 8, 132], BF16)
        for (t, tk) in V1s.bufs:
            P.memset("pool", t[:, :, 128:129], 1.0, [tk])
        pes = Rot(sb, "pe", 3, [128, 2, 128], BF16)
        PTs = Rot(sb, "PT", 3, [128, 2, 128], BF16)
        Osts = Rot(sb, "Ost", 2, [128, 8, 129], F32)
        pj = Rot(psb, "pj", 3, [128, 512], F32)
        psTs = Rot(psb, "psT", 2, [128, 1024], BF16)
        psSs = Rot(psb, "psS", 2, [128, 512], F32)
        psO_ = psb("psO", [128, 512], F32)
        nmask = 0
        for r in range(d if _stage >= 1 else 0):
            kt_hist = {}
            for j in range(NT + 1):
                cnt = 128 if j < NT else 64
                tcol = r * (NT + 1) + j
                start = 128 * j * d + r
                tsl = slice(start, start + (cnt - 1) * d + 1, d)
                have_q = j < NT
                KT, KTk = KTs.next()
                V1, V1k = V1s.next()
                kt_hist[j] = (KT, KTk, V1, V1k, cnt)
                for which in ((0, 1, 2) if have_q else (1, 2)):
                    if which == 0:
                        dstb, dstk = qbs.next()
                    elif which == 1:
                        dstb, dstk = kbs.next()
                    for half in range(2):
                        ps, pk = pj.next()
                        cb = which * 1024 + half * 512
                        for c in range(NCH):
                            P.mm(ps[0:cnt, :], xn[:, c, tsl], W[:, c, cb:cb + 512], c == 0, c == NCH - 1,
                                 [XK[c]] + WK, [pk])
                        ps3 = ps[0:cnt, :].rearrange("p (h e) -> p h e", e=128)
                        if _stage < 2:
                            continue
                        if which == 2:
                            P.act(V1[0:cnt, half * 4:(half + 1) * 4, 0:128], ps3, AF.Copy, [pk], [V1k])
                            continue
                        d3 = dstb[0:cnt, half * 512:(half + 1) * 512].rearrange("p (h e) -> p h e", e=128)
                        P.act(d3[:, :, 32:128], ps3[:, :, 32:128], AF.Copy, [pk], [dstk + "m%d" % half])
                        tA, tAk = tAs.next()
                        tB, tBk = tBs.next()
                        xr, xrk = xrs.next()
                        P.act(xr[0:cnt], ps3[:, :, 0:32], AF.Copy, [pk], [xrk])
                        pk = xrk
                        ps3 = xr[0:cnt]
                        P.tt("dve", tA[0:cnt], ps3[:, :, 0:32],
                             CS[0:cnt, tcol, :].unsqueeze(1).to_broadcast([cnt, 4, 32]), ALU.mult, [pk, "CS"], [tAk])
                        P.tt("dve", tB[0:cnt, :, 0:16], ps3[:, :, 16:32],
                             SN[0:cnt, tcol, 0:16].unsqueeze(1).to_broadcast([cnt, 4, 16]), ALU.mult, [pk, "SN"], [tBk])
                        P.tt("dve", tB[0:cnt, :, 16:32], ps3[:, :, 0:16],
                             SN[0:cnt, tcol, 16:32].unsqueeze(1).to_broadcast([cnt, 4, 16]), ALU.mult, [pk, "SN"], [tBk])
                        P.tt("dve" if os.environ.get("DBG_X") != "7" else "pool", d3[:, :, 0:32], tA[0:cnt], tB[0:cnt], ALU.add, [tAk, tBk], [dstk + "r%d" % half])
                    if which == 2 or _stage < 3:
                        continue
                    DK = [dstk + "m0", dstk + "m1", dstk + "r0", dstk + "r1"]
                    pT, pTk = psTs.next()
                    pT3 = pT[:].rearrange("p (h e) -> p h e", e=128)
                    for h in range(8):
                        P.tr(pT3[:, h, :], dstb[:, h * 128:(h + 1) * 128], ident[:], DK + ["ident"], [pTk])
                    if which == 0:
                        slot = j % 4
                        P.copy("act", QT[:, :, slot * 128:(slot + 1) * 128], pT3[:, :, :], [pTk], ["QT%d" % slot])
                        if slot == 0:
                            P.copy("pool", QT[:, :, 512:640], QT[:, :, 0:128], ["QT0"], ["QT4"])
                    else:
                        P.copy("act", KT[:], pT3[:, :, :], [pTk], [KTk])
                pieces = []
                if j == 0:
                    pieces.append((-1, 64, 128))
                else:
                    pieces.append((j - 1, 0, 128 if j < NT else 64))
                if _stage < 4:
                    pieces = []
                for (qp, l0, l1) in pieces:
                    nq = l1 - l0
                    u0 = 128 * qp + 64 + l0
                    ring = ((u0 // 128) % 4) * 128 + (u0 % 128)
                    s0, s1 = (u0 // 128) % 4, ((u0 + nq - 1) // 128) % 4
                    QK = ["QT%d" % s0] + (["QT%d" % (4 if s1 == 0 and s0 == 3 else s1)] if s1 != s0 else [])
                    parts = []
                    if qp >= 0:
                        parts.append((0,) + kt_hist[qp])
                    parts.append((1,) + kt_hist[qp + 1])
                    Ost, Ostk = Osts.next()
                    for h in range(8):
                        pS, pSk = psSs.next()
                        pS3 = pS[:, 0:256].rearrange("p (a b) -> p a b", b=128)
                        for (pi, KT_, KTk_, V1_, V1k_, cnt_) in parts:
                            P.mm(pS3[0:cnt_, pi, 0:nq], KT_[:, h, 0:cnt_], QT[:, h, ring:ring + nq], True, True,
                                 [KTk_] + QK, [pSk])
                        pe_, pek = pes.next()
                        PT, PTk = PTs.next()
                        full = len(parts) == 2 and all(p[5] == 128 for p in parts)
                        meng = "pool" if nmask % 3 == 2 else "dve"
                        nmask += 1
                        if full:
                            P.act(pe_[:, :, 0:nq], pS3[:, :, 0:nq], AF.Exp, [pSk], [pek], scale=scale)
                            P.tt(meng, PT[:, :, 0:nq], pe_[:, :, 0:nq], mask3[:, :, l0:l1], ALU.mult, [pek, "mask"], [PTk])
                        else:
                            for (pi, KT_, KTk_, V1_, V1k_, cnt_) in parts:
                                P.act(pe_[0:cnt_, pi, 0:nq], pS3[0:cnt_, pi, 0:nq], AF.Exp, [pSk], [pek], scale=scale)
                                P.tt(meng, PT[0:cnt_, pi, 0:nq], pe_[0:cnt_, pi, 0:nq], mask3[0:cnt_, pi, l0:l1], ALU.mult,
                                     [pek, "mask"], [PTk])
                        ob = (h % 3) * 129
                        ok = "psO%d" % (h % 3)
                        for n_, (pi, KT_, KTk_, V1_, V1k_, cnt_) in enumerate(parts):
                            P.mm(psO_[0:nq, ob:ob + 129], PT[0:cnt_, pi, 0:nq], V1_[0:cnt_, h, 0:129], n_ == 0,
                                 n_ == len(parts) - 1, [PTk, V1k_], [ok])
                        P.copy("act" if h % 2 == 0 else "dve", Ost[0:nq, h, :], psO_[0:nq, ob:ob + 129], [ok], [Ostk + str(h)])
                    if _stage < 5:
                        continue
                    dst = bass.AP(self.Og[g].tensor, (u0 * d + r) * 1032, [[d * 1032, nq], [1, 1032]])
                    P.dma("sp", dst, Ost[0:nq].rearrange("p a b -> p (a b)"), [Ostk + str(h) for h in range(8)],
                          [Ostk + str(h) for h in range(8)] if False else [])
        P.emit()


def _phase_merge(self):
    nc = self.nc
    P = Phase(self.prog)
    self.arena_reset()
    N = 512
    with contextlib.ExitStack() as st:
        sb = self.arena_alloc
        psb = self.psum_alloc
        ident = sb("ident", [128, 128], BF16)
        P.dma("sp", ident[:], self.c_ident, [], ["ident"])
        stg = Rot(sb, "stg", 2, [128, D], F32)
        Wo = sb("Wo", [128, NCH, D], BF16)
        self.load_w(P, Wo, "Wo", self.b_w_out, NCH, 0, D, stg, 0)
        WOK = ["Wo_%d_0" % k for k in range(NCH)]
        Os = [Rot(sb, "O%d_" % g, 2, [128, 8, 129], F32) for g in range(3)]
        rds = Rot(sb, "rd", 2, [128, 8], F32)
        obs = Rot(sb, "ob", 2, [128, D], BF16)
        oTs = Rot(sb, "oT", 2, [128, NCH, N], BF16)
        hts = Rot(sb, "ht", 2, [128, NCH, N], F32)
        psTs = Rot(psb, "psT", 2, [128, 1024], BF16)
        pj = Rot(psb, "pj", 3, [128, 512], F32)
        for sti in range(OWN // N):
            t0 = sti * N
            ht, hk = hts.next()
            okeys = [hk + "o%d" % i for i in range(NCH)]
            P.dma("sp", ht[:, 0:4, :], self.h2T[0:4, :, t0:t0 + N].rearrange("c p t -> p c t"), [], [hk + "a"] + okeys[0:4])
            P.dma("pool", ht[:, 4:8, :], self.h2T[4:8, :, t0:t0 + N].rearrange("c p t -> p c t"), [], [hk + "b"] + okeys[4:8])
            HK = [hk + "a", hk + "b"]
            oT, oTk = oTs.next()
            for ti in range(4):
                tt0 = t0 + ti * 128
                tl = []
                for g in range(3):
                    o_, ok_ = Os[g].next()
                    P.dma("sp" if g != 1 else "pool", o_[:].rearrange("p a b -> p (a b)"), self.Og[g][tt0:tt0 + 128, :], [], [ok_])
                    tl.append((o_, ok_))
                (o0, k0), (o1, k1), (o2, k2) = tl
                P.tt("pool", o0[:], o0[:], o1[:], ALU.add, [k0, k1], [k0])
                P.tt("dve", o0[:], o0[:], o2[:], ALU.add, [k0, k2], [k0])
                rd, rdk = rds.next()
                P.add("dve", lambda e, rd=rd, o0=o0: e.reciprocal(out=rd[:], in_=o0[:, :, 128]), [k0], [rdk])
                ob, obk = obs.next()
                P.tt("dve", ob[:].rearrange("p (h e) -> p h e", e=128), o0[:, :, 0:128],
                     rd[:].unsqueeze(2).to_broadcast([128, 8, 128]), ALU.mult, [k0, rdk], [obk])
                pT, pTk = psTs.next()
                pT3 = pT[:].rearrange("p (h e) -> p h e", e=128)
                for h in range(8):
                    P.tr(pT3[:, h, :], ob[:, h * 128:(h + 1) * 128], ident[:], [obk, "ident"], [pTk])
                P.copy("act", oT[:, :, ti * 128:(ti + 1) * 128], pT3, [pTk], [oTk + str(ti)])
            OK_ = [oTk + str(i) for i in range(4)]
            for oc in range(NCH):
                ps, pk = pj.next()
                for c in range(NCH):
                    P.mm(ps[:], Wo[:, c, oc * 128:(oc + 1) * 128], oT[:, c, :], c == 0, c == NCH - 1, WOK + OK_, [pk])
                P.tt("dve", ht[:, oc, :], ps[:], ht[:, oc, :], ALU.add, HK + [pk], [okeys[oc]])
                P.dma("sp" if oc % 2 == 0 else "pool", self.h3T[oc, :, t0:t0 + N], ht[:, oc, :], [okeys[oc]], [])
        P.emit()


K.phase_attn = _phase_attn
K.phase_merge = _phase_merge

CORES_PER_LAUNCH = 8

ALL_PHASES = [
    ("phase_mlstm", "A"), ("phase_mlstm", "B"),
    ("phase_ffn", 0, "h1T", "yT", EXT),
    ("phase_ple", 0, "yT", EXT, "h2T", OWN, "xn1T", ("norm_mix", 1), "bf16"),
    ("phase_attn", 0), ("phase_attn", 1), ("phase_attn", 2),
    ("phase_merge",),
    ("phase_ffn", 1, "h3T", "y2T", OWN),
    ("phase_ple", 1, "y2T", OWN, None, 0, "outT", ("final_norm",), "f32"),
]


def kernel(**inputs):
    inputs = {k: np.asarray(v) for k, v in inputs.items()}
    nc, k_ = build()
    maps = [{n: v for n, v in m.items() if n in k_.declared} for m in make_in_maps(inputs, range(8))]
    results = []
    for i in range(0, 8, CORES_PER_LAUNCH):
        res = run_bass_kernel_spmd(nc, maps[i:i + CORES_PER_LAUNCH], core_ids=list(range(CORES_PER_LAUNCH)))
        results.extend(res.results)
    out = np.empty((4, SEQ, D), np.float32)
    for c in range(8):
        b, odd = c // 2, c % 2
        o = np.asarray(results[c]["outT"]).reshape(D, OWN).T
        if odd:
            out[b, SEQ - 1 - np.arange(OWN)] = o
        else:
            out[b, :OWN] = o
    return out
```
